# Optimizing a Trainium2 kernel written in Bass

```python
import math
import jax, jax.numpy as jnp
from jax import lax
import numpy as np

D_MODEL = 1024
BATCH = 2
SEQ = 8192
DEPTH = 1
DEC_BATCH = 128
DEC_SEQ = 4
PAST_LEN = 8192
PAGE_SIZE = 128

ATT_HEAD_DIM = 64
ATT_HEADS = (D_MODEL // 2) // ATT_HEAD_DIM
ATT_KV_HEADS = ATT_HEADS // 4
ATT_GROUP = ATT_HEADS // ATT_KV_HEADS
ATT_WIDTH = ATT_HEADS * ATT_HEAD_DIM
WINDOW = 128
NUM_BUCKETS = 32
MAX_DISTANCE = 128
HG_KEY = 128
HG_VAL = 128
HG_HEADS = (D_MODEL // 2) // HG_VAL
HG_WIDTH = HG_HEADS * HG_VAL
HG_FDIM = HG_HEADS * HG_KEY
HG_CHUNK = 64
MIX_WIDTH = ATT_WIDTH + HG_WIDTH
D_FF = ((8 * D_MODEL // 3 + 127) // 128) * 128
CONV_W = 3
RMS_EPS = 1e-6
SPLITS = (ATT_WIDTH, ATT_KV_HEADS * ATT_HEAD_DIM, ATT_KV_HEADS * ATT_HEAD_DIM,
          HG_FDIM, HG_FDIM, HG_WIDTH, HG_WIDTH)
PROJ_WIDTH = sum(SPLITS)
SPLIT_IDX = [sum(SPLITS[:i + 1]) for i in range(len(SPLITS) - 1)]

kernel_name = 'hymba_swa_sink_hgrn2_convffn_step'


def rms_norm(x, g):
    xf = x.astype(jnp.float32)
    y = xf * lax.rsqrt(jnp.mean(xf * xf, axis=-1, keepdims=True) + RMS_EPS)
    return (y * g.astype(jnp.float32)).astype(x.dtype)


def t5_bucket(dist):
    n = jnp.maximum(dist, 0)
    max_exact = NUM_BUCKETS // 2
    nf = jnp.maximum(n, 1).astype(jnp.float32)
    large = max_exact + (jnp.log(nf / max_exact) / math.log(MAX_DISTANCE / max_exact)
                         * (NUM_BUCKETS - max_exact)).astype(jnp.int32)
    large = jnp.minimum(large, NUM_BUCKETS - 1)
    return jnp.where(n < max_exact, n, large)


def sink_window_attention(q, k, v, dist, valid, rel_bias, sinks):
    nq, nk = dist.shape
    s = jnp.einsum('bnqhgd,bnshd->bnhgqs', q.astype(jnp.float32), k.astype(jnp.float32)) * (ATT_HEAD_DIM ** -0.5)
    bias = rel_bias.astype(jnp.float32)[t5_bucket(dist)]
    bias = bias.reshape(nq, nk, ATT_KV_HEADS, ATT_GROUP).transpose(2, 3, 0, 1)
    mask = valid[:, None, None] & ((dist >= 0) & (dist < WINDOW))
    s = jnp.where(mask, s + bias, -jnp.inf)
    sink = sinks.astype(jnp.float32).reshape(ATT_KV_HEADS, ATT_GROUP)[:, :, None]
    m = jnp.maximum(jnp.max(s, axis=-1), sink)
    p = jnp.exp(s - m[..., None])
    den = jnp.sum(p, axis=-1) + jnp.exp(sink - m)
    o = jnp.einsum('bnhgqs,bnshd->bnqhgd', p, v.astype(jnp.float32))
    return o / den.transpose(0, 1, 4, 2, 3)[..., None]


def attn_prompt(q, k, v, rel_bias, sinks):
    b, s_len = q.shape[:2]
    nb = s_len // WINDOW
    qb = q.reshape(b, nb, WINDOW, ATT_KV_HEADS, ATT_GROUP, ATT_HEAD_DIM)

    def band(t):
        tp = jnp.pad(t, ((0, 0), (WINDOW, 0), (0, 0), (0, 0)))
        tp = tp.reshape(b, nb + 1, WINDOW, ATT_KV_HEADS, ATT_HEAD_DIM)
        return jnp.concatenate([tp[:, :-1], tp[:, 1:]], axis=2)

    qi = jnp.arange(WINDOW)[:, None]
    si = jnp.arange(2 * WINDOW)[None, :]
    dist = WINDOW + qi - si
    valid = (jnp.arange(nb)[:, None, None] * WINDOW - WINDOW + si[None]) >= 0
    o = sink_window_attention(qb, band(k), band(v), dist, valid, rel_bias, sinks)
    return o.reshape(b, s_len, ATT_WIDTH)


def attn_sample(q, k, v, k_buf, v_buf, rel_bias, sinks):
    b, l = q.shape[:2]
    wb = k_buf.shape[1]
    kc = jnp.concatenate([k_buf.astype(k.dtype), k], axis=1)
    vc = jnp.concatenate([v_buf.astype(v.dtype), v], axis=1)
    dist = jnp.arange(l)[:, None] + wb - jnp.arange(wb + l)[None, :]
    valid = jnp.ones((1, 1, wb + l), dtype=bool)
    qb = q.reshape(b, 1, l, ATT_KV_HEADS, ATT_GROUP, ATT_HEAD_DIM)
    o = sink_window_attention(qb, kc[:, None], vc[:, None], dist, valid, rel_bias, sinks)
    return o.reshape(b, l, ATT_WIDTH), kc[:, -wb:], vc[:, -wb:]


def hgrn2_recurrence(q, logf, k, v, s0):
    b, l, h, dk = q.shape
    c = HG_CHUNK if l % HG_CHUNK == 0 else l
    nc = l // c

    def to_chunks(t):
        return t.astype(jnp.float32).reshape(b, nc, c, h, t.shape[-1]).transpose(1, 0, 3, 2, 4)

    causal = jnp.tril(jnp.ones((c, c), dtype=bool))

    def step(s, xs):
        qc, lfc, kc, vc = xs
        cb = jnp.cumsum(lfc, axis=2)
        o = jnp.einsum('bhtk,bhkv->bhtv', qc * jnp.exp(cb), s)
        decay = jnp.exp(jnp.where(causal[:, :, None], cb[:, :, :, None, :] - cb[:, :, None, :, :], -jnp.inf))
        a = jnp.einsum('bhtk,bhtsk,bhsk->bhts', qc, decay, kc)
        o = o + jnp.einsum('bhts,bhsv->bhtv', a, vc)
        bl = cb[:, :, -1:, :]
        s = jnp.exp(bl[:, :, 0])[..., None] * s + jnp.einsum('bhsk,bhsv->bhkv', kc * jnp.exp(bl - cb), vc)
        return s, o

    s_fin, o = lax.scan(step, s0.astype(jnp.float32), (to_chunks(q), to_chunks(logf), to_chunks(k), to_chunks(v)))
    o = o.transpose(1, 0, 3, 2, 4).reshape(b, l, h, v.shape[-1])
    return o, s_fin


def mixer(h, w_in, w_out, sinks, attn_g, hg_g, lb, rel_bias, k_buf, v_buf, s0):
    b, l = h.shape[:2]
    qa, ka, va, qh, fh, ih, gh = jnp.split(h @ w_in, SPLIT_IDX, axis=-1)
    qa = qa.reshape(b, l, ATT_HEADS, ATT_HEAD_DIM)
    ka = ka.reshape(b, l, ATT_KV_HEADS, ATT_HEAD_DIM)
    va = va.reshape(b, l, ATT_KV_HEADS, ATT_HEAD_DIM)
    if k_buf is None:
        o_a = attn_prompt(qa, ka, va, rel_bias, sinks)
        wb = min(WINDOW, l)
        k_new, v_new = ka[:, -wb:], va[:, -wb:]
        s0 = jnp.zeros((b, HG_HEADS, HG_KEY, HG_VAL), jnp.float32)
    else:
        o_a, k_new, v_new = attn_sample(qa, ka, va, k_buf, v_buf, rel_bias, sinks)
    q_h = jax.nn.silu(qh.astype(jnp.float32)).reshape(b, l, HG_HEADS, HG_KEY)
    f = lb + (1.0 - lb) * jax.nn.sigmoid(fh.astype(jnp.float32))
    logf = jnp.log(f).reshape(b, l, HG_HEADS, HG_KEY)
    k_h = (1.0 - f).reshape(b, l, HG_HEADS, HG_KEY)
    v_h = ih.astype(jnp.float32).reshape(b, l, HG_HEADS, HG_VAL)
    o_h, s_new = hgrn2_recurrence(q_h, logf, k_h, v_h, s0)
    o_h = o_h * lax.rsqrt(jnp.mean(o_h * o_h, axis=-1, keepdims=True) + RMS_EPS)
    o_h = o_h.reshape(b, l, HG_WIDTH) * hg_g.astype(jnp.float32) * jax.nn.silu(gh.astype(jnp.float32))
    o_a = rms_norm(o_a, attn_g)
    out = jnp.concatenate([o_a, o_h], axis=-1).astype(h.dtype) @ w_out
    return out, k_new, v_new, s_new


def conv_ffn(h, w_in, conv_w, conv_b, w_out, prev):
    b, l = h.shape[:2]
    a, g = jnp.split(h @ w_in, 2, axis=-1)
    if prev is None:
        prev = jnp.zeros((b, CONV_W - 1, D_FF), a.dtype)
    ap = jnp.concatenate([prev.astype(a.dtype), a], axis=1)
    ac = conv_b + ap[:, 0:l] * conv_w[0]
    for j in range(1, CONV_W):
        ac = ac + ap[:, j:j + l] * conv_w[j]
    y = (jax.nn.silu(ac) * g) @ w_out
    return y, ap[:, -(CONV_W - 1):]


def layer(x, lp, lb, rel_bias, k_buf, v_buf, s0, conv_prev):
    n1, w_in, sinks, attn_g, hg_g, w_o, n2, wf_in, cw, cb, wf_out = lp
    mix, k_new, v_new, s_new = mixer(rms_norm(x, n1), w_in, w_o, sinks, attn_g, hg_g, lb, rel_bias, k_buf, v_buf, s0)
    x = x + mix
    ff, c_new = conv_ffn(rms_norm(x, n2), wf_in, cw, cb, wf_out, conv_prev)
    x = x + ff
    return x, k_new, v_new, s_new, c_new


def setup_inputs(seed: int = 0) -> dict:
    key = jax.random.key(seed)
    ks = jax.random.split(key, 24)
    f32 = jnp.float32

    def nrm(k, shape, scale):
        return jax.random.normal(k, shape, f32) * scale

    win = min(WINDOW, PAST_LEN)
    return {
        'x_prompt': nrm(ks[0], (BATCH, SEQ, D_MODEL), 1.0),
        'x_sample': nrm(ks[1], (DEC_BATCH, DEC_SEQ, D_MODEL), 1.0),
        'cache_k_win': nrm(ks[2], (DEPTH, DEC_BATCH, win, ATT_KV_HEADS, ATT_HEAD_DIM), 1.0),
        'cache_v_win': nrm(ks[3], (DEPTH, DEC_BATCH, win, ATT_KV_HEADS, ATT_HEAD_DIM), 1.0),
        'state_hgrn': nrm(ks[4], (DEPTH, DEC_BATCH, HG_HEADS, HG_KEY, HG_VAL), 0.3),
        'state_conv': nrm(ks[5], (DEPTH, DEC_BATCH, CONV_W - 1, D_FF), 1.0),
        'norm1_g': 1.0 + nrm(ks[6], (DEPTH, D_MODEL), 0.01),
        'w_in': nrm(ks[7], (DEPTH, D_MODEL, PROJ_WIDTH), D_MODEL ** -0.5),
        'attn_sinks': nrm(ks[8], (DEPTH, ATT_HEADS), 0.5),
        'rel_bias': nrm(ks[9], (NUM_BUCKETS, ATT_HEADS), 0.1),
        'lb_gamma': nrm(ks[10], (DEPTH + 1, HG_FDIM), 0.1),
        'attn_out_g': 1.0 + nrm(ks[11], (DEPTH, ATT_WIDTH), 0.01),
        'hg_out_g': 1.0 + nrm(ks[12], (DEPTH, HG_WIDTH), 0.01),
        'w_out': nrm(ks[13], (DEPTH, MIX_WIDTH, D_MODEL), MIX_WIDTH ** -0.5),
        'norm2_g': 1.0 + nrm(ks[14], (DEPTH, D_MODEL), 0.01),
        'w_ffn_in': nrm(ks[15], (DEPTH, D_MODEL, 2 * D_FF), D_MODEL ** -0.5),
        'conv_w': nrm(ks[16], (DEPTH, CONV_W, D_FF), CONV_W ** -0.5),
        'conv_b': nrm(ks[17], (DEPTH, D_FF), 0.01),
        'w_ffn_out': nrm(ks[18], (DEPTH, D_FF, D_MODEL), D_FF ** -0.5),
        'final_g': 1.0 + nrm(ks[19], (D_MODEL,), 0.01),
    }


def reference(x_prompt, x_sample, cache_k_win, cache_v_win, state_hgrn, state_conv,
              norm1_g, w_in, attn_sinks, rel_bias, lb_gamma, attn_out_g, hg_out_g, w_out,
              norm2_g, w_ffn_in, conv_w, conv_b, w_ffn_out, final_g):
    lb_all = jnp.cumsum(jax.nn.softmax(lb_gamma.astype(jnp.float32), axis=0), axis=0)
    xp, xs = x_prompt, x_sample
    kp_l, vp_l, sp_l, cp_l = [], [], [], []
    ks_l, vs_l, ss_l, cs_l = [], [], [], []
    for l in range(DEPTH):
        lp = (norm1_g[l], w_in[l], attn_sinks[l], attn_out_g[l], hg_out_g[l], w_out[l],
              norm2_g[l], w_ffn_in[l], conv_w[l], conv_b[l], w_ffn_out[l])
        xp, kp, vp, sp, cp = layer(xp, lp, lb_all[l], rel_bias, None, None, None, None)
        xs, ks_, vs_, ss_, cs_ = layer(xs, lp, lb_all[l], rel_bias, cache_k_win[l], cache_v_win[l],
                                       state_hgrn[l], state_conv[l])
        kp_l.append(kp); vp_l.append(vp); sp_l.append(sp); cp_l.append(cp)
        ks_l.append(ks_); vs_l.append(vs_); ss_l.append(ss_); cs_l.append(cs_)
    y_prompt = rms_norm(xp, final_g)
    y_sample = rms_norm(xs, final_g)
    return (y_prompt, y_sample,
            jnp.stack(kp_l), jnp.stack(vp_l), jnp.stack(sp_l), jnp.stack(cp_l),
            jnp.stack(ks_l), jnp.stack(vs_l), jnp.stack(ss_l), jnp.stack(cs_l))
```

```python
import os
import numpy as np
import concourse.bass as bass
import concourse.mybir as mybir
from concourse.bass_utils import run_bass_kernel_spmd

F32 = mybir.dt.float32
BF16 = mybir.dt.bfloat16
ALU = mybir.AluOpType
AF = mybir.ActivationFunctionType

NEG = -30000.0
EPS = 1e-6
NTOK = 2048 + 64
H0 = 12


class Prog:
    ENG = ('pe', 'act', 'dve', 'pool', 'sp')

    def __init__(self, nc, n_dma_sems=16):
        self.nc = nc
        self.stack = []
        self.ops = {e: [] for e in self.ENG}
        self.cnt = {e: 0 for e in self.ENG}
        self.sem = {}
        for e in ('pe', 'act', 'dve', 'pool'):
            self.sem[e] = self._sem('s_' + e)
        self.dma_sems = [self._sem('s_dma%d' % i) for i in range(n_dma_sems + 8)]
        self.dma_cnt = [0] * (n_dma_sems + 8)
        self.dma_pool_ids = {'sp': list(range(n_dma_sems)), 'pool': list(range(n_dma_sems, n_dma_sems + 8))}
        self.dma_rr = {'sp': 0, 'pool': 0}
        self.waited = {e: {} for e in self.ENG}
        self.last_w = {}
        self.readers = {}
        self.extra_waits = {e: [] for e in self.ENG}
        self.out_tokens = []
        self.cc_sems = []

    def _sem(self, name):
        g = self.nc.semaphore(name)
        s = g.__enter__()
        self.stack.append(g)
        return s

    def push(self):
        return len(self.stack)

    def pop_to(self, mark):
        while len(self.stack) > mark:
            self.stack.pop().__exit__(None, None, None)

    def arena_init(self, words):
        g = self.nc.sbuf_tensor("arena", [128, words], F32)
        self.arena = g.__enter__()
        self.stack.append(g)
        self.free_list = [(0, words)]
        self.allocs = {}
        self.arena_words = words

    def sbuf(self, name, shape, dtype):
        n = 1
        for d in shape[1:]:
            n *= d
        words = n if dtype == F32 else (n + 1) // 2
        words = (words + 7) // 8 * 8
        for i, (off, sz) in enumerate(self.free_list):
            if sz >= words:
                break
        else:
            raise AssertionError(("SBUF arena overflow", name, words, self.free_list))
        if sz == words:
            self.free_list.pop(i)
        else:
            self.free_list[i] = (off + words, sz - words)
        assert name not in self.allocs, name
        self.allocs[name] = (off, words)
        v = self.arena[:shape[0], off:off + words]
        if dtype != F32:
            v = v.bitcast(dtype)
        v = v[:, 0:n]
        if len(shape) == 3:
            v = v.rearrange("p (a b) -> p a b", a=shape[1])
        elif len(shape) == 4:
            v = v.rearrange("p (a b c) -> p a b c", a=shape[1], b=shape[2])
        return v

    def free(self, *names):
        for name in names:
            self.free_list.append(self.allocs.pop(name))
        self.free_list.sort()
        merged = []
        for off, sz in self.free_list:
            if merged and merged[-1][0] + merged[-1][1] == off:
                merged[-1] = (merged[-1][0], merged[-1][1] + sz)
            else:
                merged.append((off, sz))
        self.free_list = merged

    def used(self):
        return self.arena_words - sum(sz for _, sz in self.free_list)

    def psum(self, name, shape, dtype):
        g = self.nc.psum_tensor(name, list(shape), dtype)
        t = g.__enter__()
        self.stack.append(g)
        return t

    def _deps(self, eng, reads, writes):
        toks = []
        for b in reads:
            t = self.last_w.get(b)
            if t is not None:
                toks.append(t)
        for b in writes:
            t = self.last_w.get(b)
            if t is not None:
                toks.append(t)
            toks.extend(self.readers.get(b, ()))
        waits = {}
        for sem, val in self.extra_waits[eng]:
            toks.append((sem, val, None, 'x'))
        self.extra_waits[eng] = []
        for sem, val, teng, kind in toks:
            if teng == eng and kind == 'c' and eng == 'pe':
                continue
            key = id(sem)
            if self.waited[eng].get(key, 0) >= val:
                continue
            if key not in waits or waits[key][1] < val:
                waits[key] = (sem, val)
        for key, (sem, val) in waits.items():
            self.waited[eng][key] = val
        return list(waits.values())

    def _commit(self, tok, reads, writes):
        for b in reads:
            self.readers.setdefault(b, []).append(tok)
        for b in writes:
            self.last_w[b] = tok
            self.readers[b] = []

    @staticmethod
    def _excl(reads, writes):
        bk = [b for b in reads if b.startswith('bk')]
        if not bk:
            return reads, writes
        return [b for b in reads if not b.startswith('bk')], list(writes) + bk

    def op(self, eng, fn, reads=(), writes=(), sig=True):
        reads, writes = self._excl(reads, writes)
        waits = self._deps(eng, reads, writes)
        if sig:
            self.cnt[eng] += 1
            tok = (self.sem[eng], self.cnt[eng], eng, 'c')
        else:
            assert eng == 'pe'
            tok = (self.sem[eng], self.cnt[eng] + 1, eng, 'c')
        self.ops[eng].append((waits, fn, (self.sem[eng], 1) if sig else None))
        self._commit(tok, reads, writes)
        return tok

    def dma(self, eng, out, in_, reads=(), writes=(), is_output=False, **kw):
        ids = self.dma_pool_ids[eng]
        i = ids[self.dma_rr[eng] % len(ids)]
        self.dma_rr[eng] += 1
        sem = self.dma_sems[i]
        waits = self._deps(eng, reads, writes)
        prev = self.dma_cnt[i]
        if prev > 0 and self.waited[eng].get(id(sem), 0) < prev:
            waits.append((sem, prev))
            self.waited[eng][id(sem)] = prev
        self.dma_cnt[i] += 16
        tok = (sem, self.dma_cnt[i], eng, 'd')
        self.ops[eng].append((waits, lambda e: e.dma_start(out=out, in_=in_, **kw), (sem, 16)))
        self._commit(tok, reads, writes)
        if is_output:
            self.out_tokens.append(tok)
        return tok

    def collective(self, cin, cout, groups, reads, writes):
        sem = self._sem('s_cc%d' % len(self.cc_sems))
        self.cc_sems.append(sem)
        waits = self._deps('pool', reads, writes)
        tok = (sem, 1, 'pool', 'd')
        self.ops['pool'].append((waits, lambda e: e.collective_compute(
            "AllGather", ALU.bypass, replica_groups=groups, ins=[cin.ap().opt()], outs=[cout.ap().opt()]), (sem, 1)))
        self._commit(tok, reads, writes)
        return tok

    def barrier(self, skip_pool_dma=False):
        toks = []
        for e in ('pe', 'act', 'dve', 'pool'):
            if self.cnt[e] > 0:
                toks.append((self.sem[e], self.cnt[e]))
        for i, s in enumerate(self.dma_sems):
            if skip_pool_dma and i in self.dma_pool_ids['pool']:
                continue
            if self.dma_cnt[i] > 0:
                toks.append((s, self.dma_cnt[i]))
        for e in self.ENG:
            self.extra_waits[e].extend(toks)

    def emit(self):
        nc = self.nc
        fin = {}
        for sem, val, _, _ in self.out_tokens:
            if id(sem) not in fin or fin[id(sem)][1] < val:
                fin[id(sem)] = (sem, val)
        final_waits = list(fin.values())
        engobj = {'pe': 'tensor', 'act': 'scalar', 'dve': 'vector', 'pool': 'gpsimd', 'sp': 'sync'}
        with nc.Block() as block:
            for e in self.ENG:
                def body(engine, ops=self.ops[e], fw=(final_waits if e == 'sp' else [])):
                    for waits, fn, inc in ops:
                        for sem, val in waits:
                            engine.wait_ge(sem, val)
                        ins = fn(engine)
                        if inc is not None:
                            ins.then_inc(inc[0], inc[1])
                    for sem, val in fw:
                        engine.wait_ge(sem, val)
                getattr(block, engobj[e])(body)

    def close(self):
        self.pop_to(0)

    def mm(self, out, lhsT, rhs, start, stop, r, w, sig=True, skip=False):
        bp = lhsT.base_partition()
        if bp != 0:
            return self.op('pe', lambda e: e.matmul(out, lhsT=lhsT, rhs=rhs, start=start, stop=stop,
                                                    skip_group_check=skip, tile_position=(bp, 0)), r, w, sig)
        return self.op('pe', lambda e: e.matmul(out, lhsT=lhsT, rhs=rhs, start=start, stop=stop,
                                                skip_group_check=skip), r, w, sig)

    def tr(self, out, in_, ident, r, w):
        return self.op('pe', lambda e: e.transpose(out=out, in_=in_, identity=ident), r, w)

    def act(self, out, in_, func, r, w, scale=1.0, bias=0.0, accum=None):
        if accum is None:
            return self.op('act', lambda e: e.activation(out=out, in_=in_, func=func, scale=scale, bias=bias), r, w)
        return self.op('act', lambda e: e.activation(out=out, in_=in_, func=func, scale=scale, bias=bias,
                                                     accum_out=accum), r, w)

    def tt(self, eng, out, in0, in1, op, r, w):
        return self.op(eng, lambda e: e.tensor_tensor(out=out, in0=in0, in1=in1, op=op), r, w)

    def ts(self, eng, out, in0, s1, s2, op0, op1, r, w):
        return self.op(eng, lambda e: e.tensor_scalar(out=out, in0=in0, scalar1=s1, scalar2=s2, op0=op0, op1=op1), r, w)

    def stt(self, out, in0, scalar, in1, op0, op1, r, w):
        return self.op('dve', lambda e: e.scalar_tensor_tensor(out=out, in0=in0, scalar=scalar, in1=in1,
                                                               op0=op0, op1=op1), r, w)

    def cp(self, eng, out, in_, r, w):
        if eng == 'act':
            return self.op('act', lambda e: e.copy(out=out, in_=in_), r, w)
        return self.op(eng, lambda e: e.tensor_copy(out=out, in_=in_), r, w)

    def ms(self, eng, ap, val, w):
        return self.op(eng, lambda e: e.memset(ap, val), (), w)

    def recip(self, out, in_, r, w):
        return self.op('dve', lambda e: e.reciprocal(out=out, in_=in_), r, w)

    def scan(self, out, d0, d1, init, r, w):
        return self.op('dve', lambda e: e.tensor_tensor_scan(out=out, data0=d0, data1=d1, initial=init,
                                                             op0=ALU.mult, op1=ALU.add), r, w)


def bc(ap, shape):
    return ap.to_broadcast(list(shape))


def dup_cols(ap2):
    a = ap2.ap
    return bass.AP(ap2.tensor, ap2.offset, [list(a[0]), [0, 2], list(a[-1])])


PARTS = [(0, 8), (8, 16), (16, 22)]


def build_program(stop=None):
    nc = bass.Bass("TRN2", target_bir_lowering=False)

    def din(name, shape):
        return nc.dram_tensor(name, list(shape), F32, kind="ExternalInput")

    def dout(name, shape):
        return nc.dram_tensor(name, list(shape), F32, kind="ExternalOutput")

    xp = din("xp", [2048, 1024]); xh = din("xh", [128, 1024]); xs = din("xs", [64, 1024])
    ck = din("ck", [16, 128, 128]); cv = din("cv", [16, 128, 128])
    sh = din("sh", [16, 4, 128, 128]); scv = din("scv", [32, 2816])
    w_in = din("w_in", [1024, 2816]); w_out = din("w_out", [1024, 1024])
    w_fi = din("w_fi", [1024, 5632]); w_fo = din("w_fo", [2816, 1024])
    vecs = din("vecs", [128, 120]); fgv = din("fgv", [1, 1024]); sinks = din("sinks", [1, 8])
    relb = din("relb", [32, 8]); flags = din("flags", [1, 16])
    c_ident = din("c_ident", [128, 128]); c_onehot = din("c_onehot", [32, 128])
    c_maskbd = din("c_maskbd", [128, 128]); c_masks = din("c_masks", [64, 64])
    c_bdneg = din("c_bdneg", [64, 64]); c_scanm = din("c_scanm", [1, 512]); c_rowm = din("c_rowm", [64, 16])

    y_p = dout("y_p", [2048, 1024]); y_s = dout("y_s", [64, 1024])
    nk_p = dout("nk_p", [128, 128]); nv_p = dout("nv_p", [128, 128])
    nh_p = dout("nh_p", [4, 128, 128]); nc_p = dout("nc_p", [2, 2816])
    nk_s = dout("nk_s", [16, 128, 128]); nv_s = dout("nv_s", [16, 128, 128])
    nh_s = dout("nh_s", [16, 4, 128, 128]); nc_s = dout("nc_s", [32, 2816])

    scr = nc.dram_tensor("scr_toep", [8 * 128 * 383], F32, kind="Internal")
    cc1_in = nc.dram_tensor("cc1_in", [128, 516], F32, kind="Internal")
    cc1_out = nc.dram_tensor("cc1_out", [4 * 128, 516], F32, kind="Internal")
    cc2_in = [nc.dram_tensor("cc2_in%d" % i, [128, 16], F32, kind="Internal") for i in range(3)]
    cc2_out = [nc.dram_tensor("cc2_out%d" % i, [4 * 128, 16], F32, kind="Internal") for i in range(3)]
    GROUPS = [[0, 1, 2, 3], [4, 5, 6, 7]]

    P = Prog(nc)
    P.arena_init(53000)

    def finish():
        P.emit()
        P.close()
        return nc

    bank_f = [P.psum("bk%d" % i, [128, 512], F32) for i in range(8)]
    bank_b = [b[:].bitcast(BF16) for b in bank_f]
    BK = lambda i: 'bk%d' % i

    ident = P.sbuf("ident", [128, 128], BF16)
    ident_f = P.sbuf("ident_f", [128, 128], F32)
    ones_bf = P.sbuf("ones_bf", [128, 128], BF16)
    zeros_bf = P.sbuf("zeros_bf", [128, 512], BF16)
    vec = P.sbuf("vec", [128, 120], F32)
    fl = P.sbuf("fl", [128, 16], F32)
    stat = P.sbuf("stat", [128, 8], F32)
    onesf8 = P.sbuf("onesf8", [128, 16], F32)
    g1T = vec[:, 0:8]; g2T = vec[:, 8:16]; aogT = vec[:, 16:20]; hggT = vec[:, 20:24]
    lbg0 = vec[:, 24:28]; lbg1 = vec[:, 28:32]
    cwT = [vec[:, 32 + 22 * j: 32 + 22 * (j + 1)] for j in range(3)]; cbT = vec[:, 98:120]
    HALO = fl[:, 0:1]
    ACTF = [fl[:, 1 + j: 2 + j] for j in range(4)]
    NACT = [fl[:, 5 + j: 6 + j] for j in range(4)]
    SEL = [fl[:, 9 + j: 10 + j] for j in range(4)]

    oaT = P.sbuf("oaT", [128, 4, NTOK], BF16)
    o_loc = P.sbuf("o_loc", [128, 4, NTOK], BF16)
    qg = P.sbuf("qg", [128, 4, NTOK], BF16)
    gate = P.sbuf("gate", [128, 4, NTOK], BF16)
    S = P.sbuf("S", [128, 4, 128], F32)
    Gtot = P.sbuf("Gtot", [128, 4], F32)
    cc1_sb = P.sbuf("cc1_sb", [128, 516], F32)

    w_bf = P.sbuf("w_bf", [128, 8, 2816], BF16)
    hT = P.sbuf("hT", [128, 8, 512], BF16)
    kT = P.sbuf("kT", [128, 2, 5 * 128], BF16)
    Vaug = P.sbuf("Vaug", [128, 5, 2, 65], BF16)
    qT = P.sbuf("qT", [128, 4, 512], BF16)
    xb = [P.sbuf("xb%d" % i, [128, 1024], F32) for i in range(2)]
    xn = P.sbuf("xn", [128, 1024], BF16)
    T8 = P.sbuf("T8", [128, 8, 256], BF16)
    T8x = P.sbuf("T8x", [128, 8, 256], BF16)
    T8n = P.sbuf("T8n", [64, 8, 64], BF16)
    esink = P.sbuf("esink", [128, 8], F32)
    lb = P.sbuf("lb", [128, 4], F32)
    ln1mlb = P.sbuf("ln1mlb", [128, 4], F32)
    maskbd = P.sbuf("maskbd", [128, 128], F32)
    masks = P.sbuf("masks", [64, 64], F32)
    scanm = P.sbuf("scanm", [128, 512], BF16)
    scanm_f = P.sbuf("scanm_f", [128, 512], F32)
    rowm = P.sbuf("rowm", [64, 16], F32)
    o_att = P.sbuf("o_att", [128, 8, 64], F32)
    PDO_ = P.sbuf("PDO0", [128, 2, 2, 512], BF16)
    PDO = [PDO_, PDO_]
    o_attn = P.sbuf("o_attn", [128, 512], BF16)
    kv_out = P.sbuf("kv_out", [128, 256], F32)

    tmp_cf = P.sbuf("tmp_cf", [128, 128], F32)
    rb = P.sbuf("rb", [32, 8], F32)
    oh = P.sbuf("oh", [32, 128], F32)
    Rm = P.sbuf("Rm", [32, 8, 128], F32)
    ones32 = P.sbuf("ones32", [32, 128], F32)
    Lb = P.sbuf("Lb", [128, 8, 383], F32)
    Tf = P.sbuf("Tf", [128, 8, 256], F32)
    bdn = P.sbuf("bdn", [64, 64], F32)

    def bcast_rows(t, n):
        return bass.AP(t, 0, [[0, 128], [1, n]])

    P.dma('sp', tmp_cf[:], c_ident.ap(), writes=['tmp_cf'])
    P.cp('dve', ident[:], tmp_cf[:], ['tmp_cf'], ['ident'])
    P.cp('dve', ident_f[:], tmp_cf[:], ['tmp_cf'], ['ident_f'])
    P.ms('pool', ones_bf[:], 1.0, ['ones_bf'])
    P.ms('pool', zeros_bf[:], 0.0, ['zeros_bf'])
    P.ms('pool', onesf8[:], 1.0, ['onesf8'])
    P.dma('sp', vec[:], vecs.ap(), writes=['vec'])
    P.dma('sp', fl[:], bcast_rows(flags, 16), writes=['fl'])

    w_in_v = w_in.ap().rearrange("(c p) n -> p c n", p=128)
    for (c0, c1, nm) in ((0, 768, 'w_bf_a'), (768, 1792, 'w_bf_b'), (1792, 2816, 'w_bf_c')):
        P.dma('pool', w_bf[:, :, c0:c1], w_in_v[:, :, c0:c1], writes=[nm])
    w_kd = P.sbuf("w_kd", [128, 8, 2, 128], BF16)
    P.dma('sp', xb[0][:128, :], xh.ap(), writes=['xb0'])
    WNAME = lambda col: 'w_bf_a' if col < 768 else ('w_bf_b' if col < 1792 else 'w_bf_c')

    if stop == 's1':
        return finish()
    P.dma('sp', maskbd[:], c_maskbd.ap(), writes=['maskbd'])
    P.dma('sp', masks[:], c_masks.ap(), writes=['masks'])
    P.dma('sp', scanm_f[:], bcast_rows(c_scanm, 512), writes=['scanm_f'])
    P.cp('dve', scanm[:], scanm_f[:], ['scanm_f'], ['scanm'])
    P.dma('sp', rowm[:], c_rowm.ap(), writes=['rowm'])
    P.dma('sp', esink[:], bcast_rows(sinks, 8), writes=['esink'])
    P.act(esink[:], esink[:], AF.Exp, ['esink'], ['esink'])
    P.tt('dve', stat[:, 0:4], lbg1, lbg0, ALU.subtract, ['vec'], ['stat'])
    P.act(stat[:, 0:4], stat[:, 0:4], AF.Exp, ['stat'], ['stat'])
    P.ts('dve', lb[:], stat[:, 0:4], 1.0, None, ALU.add, ALU.bypass, ['stat'], ['lb'])
    P.recip(lb[:], lb[:], ['lb'], ['lb'])
    P.tt('dve', stat[:, 4:8], stat[:, 0:4], lb[:], ALU.mult, ['stat', 'lb'], ['stat'])
    P.act(ln1mlb[:], stat[:, 4:8], AF.Ln, ['stat'], ['ln1mlb'])
    P.ms('pool', Vaug[:, :, :, 64:65], 1.0, ['Vaug'])
    P.ms('pool', S[:], 0.0, ['S'])
    P.ms('pool', Gtot[:], 0.0, ['Gtot'])

    if stop == 's2':
        return finish()
    P.dma('sp', rb[:], relb.ap(), writes=['rb'])
    P.dma('sp', oh[:], c_onehot.ap(), writes=['oh'])
    P.dma('sp', bdn[:], c_bdneg.ap(), writes=['bdn'])
    P.ms('pool', ones32[:], 1.0, ['ones32'])
    P.ms('pool', Lb[:], NEG * 8.0, ['Lb'])
    for h in range(8):
        P.ts('dve', Rm[:, h, :], oh[:], rb[:, h:h + 1], None, ALU.mult, ALU.bypass, ['oh', 'rb'], ['Rm'])
    for i in range(2):
        P.mm(bank_f[i][:, :], ones32[:], Rm[:, 4 * i:4 * i + 4, :], True, True, ['ones32', 'Rm'], [BK(i)])
        P.ts('dve', Lb[:, 4 * i:4 * i + 4, 127:255], bank_f[i][:, :].rearrange("p (h d) -> p h d", h=4), 8.0, None,
             ALU.mult, ALU.bypass, [BK(i), 'Lb'], ['Lb'])
    if stop == 's3':
        return finish()
    for h in range(8):
        P.dma('pool', bass.AP(scr, h * 128 * 383, [[383, 128], [1, 383]]), Lb[:, h, :], reads=['Lb'], writes=['scr%d' % h])
    for h in range(8):
        P.dma('pool', Tf[:, h, :], bass.AP(scr, h * 128 * 383 + 127, [[382, 128], [1, 256]]), reads=['scr%d' % h], writes=['Tf%d' % h])
    if stop == 's4':
        return finish()
    for kv in range(2):
        P.cp('dve', w_kd[:, :, kv, :].rearrange("p c (r d) -> p c r d", r=2),
             bc(w_bf[:, :, 512 + kv * 64: 512 + (kv + 1) * 64].unsqueeze(2), [128, 8, 2, 64]), ['w_bf_a'], ['w_kd'])
    xcnt = [0]
    FK = 0

    def fill(bank, k=FK):
        for _ in range(k):
            P.mm(bank_f[bank][:, :], zeros_bf[:, 0:128], zeros_bf[:, :], True, True, ['zeros_bf'], [BK(bank)], sig=False, skip=True)

    def rms_to_T(src_ap, nrows, gT, dst, dst_name, dst_off, xsrc_sb=None, fb=None, jb=7, junk=None):
        if xsrc_sb is None:
            i = xcnt[0] % 2
            xcnt[0] += 1
            xname = 'xb%d' % i
            P.dma('sp', xb[i][:nrows, :], src_ap, writes=[xname])
            xin = xb[i][:nrows, :]
        else:
            xin, xname = xsrc_sb
        jap, jname = junk if junk is not None else (xn, 'xn')
        P.act(jap[:nrows, :], xin, AF.Square, [xname], [jname, 'stat'], accum=stat[:nrows, 0:1])
        P.act(stat[:nrows, 1:2], stat[:nrows, 0:1], AF.Ln, ['stat'], ['stat'], scale=1.0 / 1024, bias=EPS)
        P.act(stat[:nrows, 2:3], stat[:nrows, 1:2], AF.Exp, ['stat'], ['stat'], scale=-0.5)
        P.ts('dve', xn[:nrows, :], xin, stat[:nrows, 2:3], None, ALU.mult, ALU.bypass, [xname, 'stat'], ['xn'])
        pb = bank_b[0].rearrange("p (c t) -> p c t", c=8)
        if fb is not None:
            fill(fb)
        for c in range(8):
            P.tr(pb[:, c, :nrows], xn[:nrows, c * 128:(c + 1) * 128], ident[:nrows, :nrows], ['xn', 'ident'], [BK(0)])
        P.tt('dve', dst[:, :, dst_off:dst_off + nrows], pb[:, :, :nrows], bc(gT.unsqueeze(2), [128, 8, nrows]),
             ALU.mult, [BK(0), 'vec'], [dst_name])

    def proj_fm(ps, lhs_fn, rhs_t, rhs_name, t0, n, wname, bk):
        for kc in range(8):
            P.mm(ps, lhs_fn(kc), rhs_t[:, kc, t0:t0 + n], kc == 0, kc == 7, [wname, rhs_name], [bk], sig=(kc == 7))

    def proj_tm(ps, lhs_t, lhs_name, t0, n, c0, ncols, bk):
        for kc in range(8):
            P.mm(ps, lhs_t[:, kc, t0:t0 + n], w_bf[:, kc, c0:c0 + ncols], kc == 0, kc == 7, [WNAME(c0), lhs_name], [bk],
                 sig=(kc == 7))

    def kv_block(t0, n, slot, out_kv=False):
        for kv in range(2):
            bk = 1 + kv
            proj_fm(bank_f[bk][:, :n], lambda kc, kv=kv: w_kd[:, kc, kv, :], hT, 'hT', t0, n, 'w_kd', BK(bk))
            P.cp('act', kT[:, kv, slot * 128: slot * 128 + n], bank_f[bk][:, :n], [BK(bk)], ['kT'])
        proj_tm(bank_f[3][:n, 0:256], hT, 'hT', t0, n, 512, 256, BK(3))
        P.cp('dve', Vaug[:n, slot, :, 0:64], bank_f[3][:n, 128:256].rearrange("p (k d) -> p k d", k=2), [BK(3)], ['Vaug'])
        if out_kv:
            P.cp('act', kv_out[:n, :], bank_f[3][:n, 0:256], [BK(3)], ['kv_out'])

    def attn_epilogue(nq, dst_off, fb=None):
        for hf in range(2):
            pv = bank_f[6 + hf][:nq, :].rearrange("p (h d) -> p h d", h=4)
            P.tt('dve', stat[:nq, 4:8], pv[:, :, 64], esink[:nq, 4 * hf:4 * hf + 4], ALU.add, [BK(6 + hf), 'esink'], ['stat'])
            P.recip(stat[:nq, 4:8], stat[:nq, 4:8], ['stat'], ['stat'])
            P.tt('dve', o_att[:nq, 4 * hf:4 * hf + 4, :], pv[:, :, 0:64], bc(stat[:nq, 4:8].unsqueeze(2), [nq, 4, 64]),
                 ALU.mult, [BK(6 + hf), 'stat'], ['o_att'])
        oa2 = o_att[:nq].rearrange("p h d -> p (h d)")
        P.act(o_attn[:nq, :], oa2, AF.Square, ['o_att'], ['o_attn', 'stat'], accum=stat[:nq, 0:1])
        P.act(stat[:nq, 1:2], stat[:nq, 0:1], AF.Ln, ['stat'], ['stat'], scale=1.0 / 512, bias=EPS)
        P.act(stat[:nq, 2:3], stat[:nq, 1:2], AF.Exp, ['stat'], ['stat'], scale=-0.5)
        P.ts('dve', o_attn[:nq, :], oa2, stat[:nq, 2:3], None, ALU.mult, ALU.bypass, ['o_att', 'stat'], ['o_attn'])
        pb = bank_b[0].rearrange("p (c t) -> p c t", c=8)
        if fb is not None:
            fill(fb)
        for c in range(4):
            P.tr(pb[:, c, :nq], o_attn[:nq, c * 128:(c + 1) * 128], ident[:nq, :nq], ['o_attn', 'ident'], [BK(0)])
        P.tt('dve', oaT[:, :, dst_off:dst_off + nq], pb[:, 0:4, :nq], bc(aogT.unsqueeze(2), [128, 4, nq]), ALU.mult,
             [BK(0), 'vec'], ['oaT'])

    def attn_scores(first, slot, ql, pb):
        tb = T8x if first else T8
        for kv in range(2):
            for par in range(2):
                bk = 2 + 2 * kv + par
                P.mm(bank_f[bk][:, :], ident[:], tb[:, 4 * kv + par:4 * kv + 4:2, :], True, False, ['ident', 'T8', 'T8x'], [BK(bk)],
                     sig=False, skip=True)
            for g in range(4):
                h = 4 * kv + g
                c, hf = h // 2, h % 2
                bk = 2 + 2 * kv + hf
                gi = g // 2
                qs = qT[hf * 64:(hf + 1) * 64, c, ql:ql + 128]
                P.mm(bank_f[bk][:, gi * 256:gi * 256 + 128], kT[hf * 64:(hf + 1) * 64, kv, slot * 128:(slot + 1) * 128], qs, False, False,
                     ['kT', 'qT'], [BK(bk)], sig=False, skip=True)
                P.mm(bank_f[bk][:, gi * 256 + 128:gi * 256 + 256], kT[hf * 64:(hf + 1) * 64, kv, (slot - 1) * 128:slot * 128], qs, False, g >= 2,
                     ['kT', 'qT'], [BK(bk)], sig=(g >= 2), skip=True)
            for par in range(2):
                bk = 2 + 2 * kv + par
                P.act(PDO[pb][:, kv, par, :], bank_f[bk][:, :], AF.Exp, [BK(bk)], ['PDO0_%d' % kv], scale=0.125)

    def attn_pv(slot, dst_off, pb):
        fill(1)
        for kv in range(2):
            for g in range(4):
                h = 4 * kv + g
                par, gi = g % 2, g // 2
                pv = bank_f[6 + h // 4][:, (h % 4) * 128:(h % 4) * 128 + 65]
                P.mm(pv, PDO[pb][:, kv, par, gi * 256 + 128:gi * 256 + 256], Vaug[:, slot - 1, kv, :], True, False, ['PDO0_%d' % kv, 'Vaug'],
                     [BK(6 + h // 4)], sig=False, skip=True)
                P.mm(pv, PDO[pb][:, kv, par, gi * 256:gi * 256 + 128], Vaug[:, slot, kv, :], False, True, ['PDO0_%d' % kv, 'Vaug'],
                     [BK(6 + h // 4)], skip=True)
        attn_epilogue(128, dst_off, fb=1)

    def interleave(*gens):
        gens = list(gens)
        while gens:
            for g in list(gens):
                try:
                    next(g)
                except StopIteration:
                    gens.remove(g)

    def hgrn_elem(n, tok_off, sample):
        tl = 4 if sample else 64
        nch = n // tl

        def head_gen(hh):
            if True:
                b0 = 1 if hh % 2 == 0 else 5
                yield
                f_ps = bank_f[b0]; q_ps = bank_f[b0 + 1]; g_ps = bank_f[b0 + 2]
                yield
                yield
                yield
                if not sample:
                    fill(4, 2 * FK)
                yield
                proj_fm(f_ps[:, :n], lambda kc, hh=hh: w_bf[:, kc, 1280 + hh * 128: 1280 + (hh + 1) * 128], hT, 'hT', 0, n, 'w_bf_b', BK(b0))
                yield
                proj_fm(q_ps[:, :n], lambda kc, hh=hh: w_bf[:, kc, 768 + hh * 128: 768 + (hh + 1) * 128], hT, 'hT', 0, n, 'w_bf_b', BK(b0 + 1))
                yield
                proj_fm(g_ps[:, :n], lambda kc, hh=hh: w_bf[:, kc, 2304 + hh * 128: 2304 + (hh + 1) * 128], hT, 'hT', 0, n, 'w_bf_c', BK(b0 + 2))
                yield
                tS = tA if hh % 2 == 0 else tB
                yield
                tN = 'tA' if hh % 2 == 0 else 'tB'
                yield
                ef, l1, l2, cbv, t4 = [t[:, :n] for t in tS]
                yield
                P.act(ef, f_ps[:, :n], AF.Exp, [BK(b0)], [tN + '0'], scale=-1.0)
                yield
                P.act(l1, ef, AF.Ln, [tN + '0'], [tN + '1'], bias=1.0)
                yield
                P.act(l2, ef, AF.Ln, [tN + '0', 'lb'], [tN + '2'], scale=lb[:, hh:hh + 1], bias=1.0)
                yield
                P.tt('dve', l2, l2, l1, ALU.subtract, [tN + '1', tN + '2'], [tN + '2'])
                yield
                if not sample:
                    P.scan(cbv, scanm[:, :n], l2, 0.0, ['scanm', tN + '2'], [tN + '3'])
                else:
                    l2v = l2.rearrange("p (b l) -> p b l", l=4); cb3 = cbv.rearrange("p (b l) -> p b l", l=4)
                    P.cp('dve', cb3[:, :, 0], l2v[:, :, 0], [tN + '2'], [tN + '3'])
                    for l in range(1, 4):
                        P.tt('dve', cb3[:, :, l], cb3[:, :, l - 1], l2v[:, :, l], ALU.add, [tN + '2', tN + '3'], [tN + '3'])
                yield
                P.tt('dve', t4, l1, cbv, ALU.add, [tN + '1', tN + '3'], [tN + '4'])
                yield
                P.stt(t4, f_ps[:, :n], -1.0, t4, ALU.mult, ALU.subtract, [BK(b0), tN + '4'], [tN + '4'])
                yield
                P.act(kt[:, hh, :n], t4, AF.Exp, [tN + '4', 'ln1mlb'], ['kt'], bias=ln1mlb[:, hh:hh + 1])
                yield
                cend = cbv.rearrange("p (c t) -> p c t", t=tl)[:, :, tl - 1]
                yield
                P.cp('dve', cbl[:, hh, :nch], cend, [tN + '3'], ['cbl'])
                yield
                P.act(dch[:, hh, :nch], cbl[:, hh, :nch], AF.Exp, ['cbl'], ['dch'])
                yield
                P.tt('pool', kh[:, hh, :n].rearrange("p (c t) -> p c t", t=tl), kt[:, hh, :n].rearrange("p (c t) -> p c t", t=tl),
                     bc(dch[:, hh, :nch].unsqueeze(2), [128, nch, tl]), ALU.mult, ['kt', 'dch'], ['kh'])
                yield
                P.act(ef, q_ps[:, :n], AF.Exp, [BK(b0 + 1)], [tN + '0'], scale=-1.0)
                yield
                P.act(l1, ef, AF.Ln, [tN + '0'], [tN + '1'], bias=1.0)
                yield
                P.tt('dve', l1, cbv, l1, ALU.subtract, [tN + '1', tN + '3'], [tN + '1'])
                yield
                P.act(l1, l1, AF.Exp, [tN + '1'], [tN + '1'])
                yield
                P.tt('dve', qt[:, hh, :n], q_ps[:, :n], l1, ALU.mult, [BK(b0 + 1), tN + '1'], ['qt'])
                yield
                if not sample:
                    P.scan(Gi[:, hh, :nch], onesf8[:, :nch], cbl[:, hh, :nch], Gtot[:, hh:hh + 1], ['cbl', 'Gtot', 'onesf8'], ['Gi'])
                    P.tt('dve', eG[:, hh, :nch], Gi[:, hh, :nch], cbl[:, hh, :nch], ALU.subtract, ['Gi', 'cbl'], ['eG'])
                    P.act(eG[:, hh, :nch], eG[:, hh, :nch], AF.Exp, ['eG'], ['eG'])
                    P.cp('dve', Gtot[:, hh:hh + 1], Gi[:, hh, nch - 1:nch], ['Gi'], ['Gtot'])
                    P.tt('pool', qg[:, hh, tok_off:tok_off + n].rearrange("p (c t) -> p c t", t=64),
                         qt[:, hh, :n].rearrange("p (c t) -> p c t", t=64), bc(eG[:, hh, :nch].unsqueeze(2), [128, nch, 64]), ALU.mult,
                         ['qt', 'eG'], ['qg'])
                yield
                P.act(ef, g_ps[:, :n], AF.Exp, [BK(b0 + 2)], [tN + '0'], scale=-1.0)
                yield
                P.ts('dve', ef, ef, 1.0, None, ALU.add, ALU.bypass, [tN + '0'], [tN + '0'])
                yield
                P.recip(ef, ef, [tN + '0'], [tN + '0'])
                yield
                P.stt(gate[:, hh, tok_off:tok_off + n], g_ps[:, :n], hggT[:, hh:hh + 1], ef, ALU.mult, ALU.mult, [BK(b0 + 2), tN + '0', 'vec'], ['gate'])
                yield

        interleave(head_gen(0), head_gen(1))
        interleave(head_gen(2), head_gen(3))
        for sb in range((n + 127) // 128):
            nn = min(128, n - sb * 128)
            proj_tm(bank_f[4][:nn, :], hT, 'hT', sb * 128, nn, 1792, 512, BK(4))
            P.cp('act', Vh[:nn, sb, :], bank_f[4][:nn, :], [BK(4)], ['Vh'])

    sbi = [0]

    def hgrn_pair(p, tok_off):
        c0 = p * 128
        A_ps = bank_f[5]; kT_ps = bank_b[6]; o_ps = bank_f[7]; U_ps = bank_f[4]
        for hh in range(4):
            P.mm(A_ps[:, hh * 128:(hh + 1) * 128], kt[:, hh, c0:c0 + 128], qt[:, hh, c0:c0 + 128], True, True, ['kt', 'qt'], [BK(5)],
                 sig=(hh == 3), skip=True)
        for hh in range(4):
            P.tr(kT_ps[:, hh * 128:(hh + 1) * 128], kh[:, hh, c0:c0 + 128], ident[:], ['kh', 'ident'], [BK(6)])
        P.tt('dve', Am[:], A_ps[:, :].rearrange("p (h t) -> p h t", h=4), bc(maskbd[:].unsqueeze(1), [128, 4, 128]), ALU.mult,
             [BK(5), 'maskbd'], ['Am'])
        P.cp('act', khT[:].rearrange("p h k -> p (h k)"), kT_ps[:, 0:512], [BK(6)], ['khT'])
        fill(0)
        P.mm(o_ps[:, :], zeros_bf[:, 0:128], zeros_bf[:, :], True, False, ['zeros_bf'], [BK(7)], sig=False, skip=True)
        for hh in range(4):
            P.mm(o_ps[:, hh * 128:(hh + 1) * 128], Vh[:, p, hh * 128:(hh + 1) * 128], Am[:, hh, :], False, False, ['Vh', 'Am'], [BK(7)],
                 sig=False, skip=True)
        U_bk = [4, 3]
        for ch in range(2):
            r0 = ch * 64
            for hh in range(4):
                P.mm(bank_f[U_bk[ch]][:, hh * 128:(hh + 1) * 128], khT[r0:r0 + 64, hh, :], Vh[r0:r0 + 64, p, hh * 128:(hh + 1) * 128], True, True,
                     ['khT', 'Vh'], [BK(U_bk[ch])], sig=(hh == 3), skip=True)
        for ch in range(2):
            sname = 'S_bf%d' % (sbi[0] % 2)
            sb_cur = S_bf[sbi[0] % 2]
            cidx = 2 * p + ch
            r0 = ch * 64
            fill(0)
            for hh in range(4):
                P.mm(o_ps[:, hh * 128 + r0: hh * 128 + r0 + 64], sb_cur[:, hh, :], qt[:, hh, c0 + r0:c0 + r0 + 64],
                     False, (ch == 1 and hh == 3), [sname, 'qt'], [BK(7)], sig=(ch == 1 and hh == 3), skip=True)
            P.tt('dve', S[:], S[:], bc(dch[:, :, cidx:cidx + 1], [128, 4, 128]), ALU.mult, ['S', 'dch'], ['S'])
            P.tt('dve', S[:], S[:], bank_f[U_bk[ch]][:, :].rearrange("p (h v) -> p h v", h=4), ALU.add, ['S', BK(U_bk[ch])], ['S'])
            sbi[0] += 1
            P.cp('dve', S_bf[sbi[0] % 2][:], S[:], ['S'], ['S_bf%d' % (sbi[0] % 2)])
        P.cp('act', o_loc[:, :, tok_off + c0: tok_off + c0 + 128], o_ps[:, :].rearrange("p (h t) -> p h t", h=4), [BK(7)], ['o_loc'])

    rms_to_T(None, 128, g1T, hT, 'hT', 0, xsrc_sb=(xb[0][:128, :], 'xb0'))
    xcnt[0] = 1
    if stop == 'h0n':
        return finish()
    kv_block(0, 128, 0)

    def tile_front(tt):
        for sb in range(4):
            r0 = tt * 512 + sb * 128
            rms_to_T(xp.ap()[r0:r0 + 128, :], 128, g1T, hT, 'hT', sb * 128, jb=1,
                     junk=((T8x[:].rearrange("p h q -> p (h q)")[:, 0:1024], 'T8x') if tt >= 1 else None))
        for c in range(4):
            bk = 4 + c % 2
            proj_fm(bank_f[bk][:, :], lambda kc, c=c: w_bf[:, kc, c * 128:(c + 1) * 128], hT, 'hT', 0, 512, 'w_bf_a', BK(bk))
            P.cp('act' if c % 2 else 'dve', qT[:, c, :], bank_f[bk][:, :], [BK(bk)], ['qT'])
        for sb in range(4):
            kv_block(sb * 128, 128, 1 + sb, out_kv=(tt == 3 and sb == 3))
        if tt == 3:
            P.dma('sp', nk_p.ap(), kv_out[:, 0:128], reads=['kv_out'], is_output=True)
            P.dma('sp', nv_p.ap(), kv_out[:, 128:256], reads=['kv_out'], is_output=True)

    tile_front(0)
    P.cp('dve', T8[:], Tf[:], ['Tf%d' % h_ for h_ in range(8)], ['T8'])
    P.cp('dve', T8x[:, :, 0:128], Tf[:, :, 0:128], ['Tf%d' % h_ for h_ in range(8)], ['T8x'])
    P.ts('dve', T8x[:, :, 128:256], Tf[:, :, 128:256], HALO, None, ALU.add, ALU.bypass, ['Tf%d' % h_ for h_ in range(8)] + ['fl', 'T8x'], ['T8x'])
    for h in range(8):
        P.tt('dve', T8n[:, h, :], Tf[0:64, h, 0:64], bdn[:], ALU.add, ['Tf%d' % h, 'bdn'], ['T8n'])
    P.barrier(skip_pool_dma=True)
    P.free('tmp_cf', 'rb', 'oh', 'Rm', 'ones32', 'Lb', 'Tf', 'bdn', 'scanm_f')

    if stop == 'setup':
        return finish()
    Vh = P.sbuf("Vh", [128, 4, 512], BF16)
    tA = [P.sbuf("tA%d" % i, [128, 512], F32) for i in range(5)]
    tB = [P.sbuf("tB%d" % i, [128, 512], F32) for i in range(5)]
    kt = P.sbuf("kt", [128, 4, 512], BF16)
    kh = P.sbuf("kh", [128, 4, 512], BF16)
    qt = P.sbuf("qt", [128, 4, 512], BF16)
    dch = P.sbuf("dch", [128, 4, 16], F32)
    eG = P.sbuf("eG", [128, 4, 8], F32)
    Gi = P.sbuf("Gi", [128, 4, 8], F32)
    cbl = P.sbuf("cbl", [128, 4, 16], F32)
    Am = P.sbuf("Am", [128, 4, 128], BF16)
    khT = P.sbuf("khT", [128, 4, 128], BF16)
    S_bf = [P.sbuf("S_bf%d" % i, [128, 4, 128], BF16) for i in range(2)]

    P.cp('act', S_bf[0][:], S[:], ['S'], ['S_bf0'])

    for tt in range(4):
        if tt > 0:
            tile_front(tt)
        for sb in range(4):
            attn_scores(tt == 0 and sb == 0, 1 + sb, sb * 128, 0)
            attn_pv(1 + sb, tt * 512 + sb * 128, 0)
        P.cp('act', kT[:, :, 0:128], kT[:, :, 512:640], ['kT'], ['kT'])
        P.cp('pool', Vaug[:, 0, :, :], Vaug[:, 4, :, :], ['Vaug'], ['Vaug'])
        if stop == 't%da' % tt:
            return finish()
        hgrn_elem(512, tt * 512, False)
        if stop == 't%de' % tt:
            return finish()
        for p in range(4):
            hgrn_pair(p, tt * 512)
        if stop == 't%dh' % tt:
            return finish()

    if stop == 'p1':
        return finish()
    P.cp('dve', cc1_sb[:, 0:512], S[:].rearrange("p h v -> p (h v)"), ['S'], ['cc1_sb'])
    P.act(cc1_sb[:, 512:516], Gtot[:], AF.Exp, ['Gtot'], ['cc1_sb'])
    P.dma('pool', cc1_in.ap(), cc1_sb[:], reads=['cc1_sb'], writes=['cc1_in'])
    P.collective(cc1_in, cc1_out, GROUPS, ['cc1_in'], ['cc1_out'])

    if stop == 'ex1':
        return finish()
    SO = 2048
    P.barrier()
    P.free('PDO0', 'kv_out', 'Am', 'khT', 'S_bf0', 'S_bf1', 'T8x', 'Gi', 'eG')
    kvn = P.sbuf("kvn", [64, 256], F32)
    ckb = [P.sbuf("ckb%d" % i, [128, 128], F32) for i in range(3)]
    ckbf = [P.sbuf("ckbf%d" % i, [128, 2, 2, 64], BF16) for i in range(3)]
    cvb = [P.sbuf("cvb%d" % i, [128, 128], F32) for i in range(3)]
    cva = [P.sbuf("cva%d" % i, [128, 2, 65], BF16) for i in range(3)]
    kcT = [P.sbuf("kcT%d" % i, [128, 2, 128], BF16) for i in range(3)]
    Pcm = [P.sbuf("Pcm%d" % i, [128, 8, 64], BF16) for i in range(2)]
    Pn = P.sbuf("Pn", [64, 8, 64], BF16)

    rms_to_T(xs.ap(), 64, g1T, hT, 'hT', 0)
    for c in range(4):
        bk = 4 + c % 2
        proj_fm(bank_f[bk][:, :64], lambda kc, c=c: w_bf[:, kc, c * 128:(c + 1) * 128], hT, 'hT', 0, 64, 'w_bf_a', BK(bk))
        P.cp('act' if c % 2 else 'dve', qT[:, c, 0:64], bank_f[bk][:, :64], [BK(bk)], ['qT'])
    kv_block(0, 64, 1)
    P.cp('act', kvn[:], bank_f[3][:64, 0:256], [BK(3)], ['kvn'])
    if stop == 'sa0':
        return finish()
    dstk = nk_s.ap().rearrange("b s d -> (b s) d")
    dstv = nv_s.ap().rearrange("b s d -> (b s) d")
    P.dma('sp', nk_s.ap()[:, 0:124, :], ck.ap()[:, 4:128, :], is_output=True)
    P.dma('sp', nv_s.ap()[:, 0:124, :], cv.ap()[:, 4:128, :], is_output=True)
    for b in range(16):
        P.dma('sp', dstk[b * 128 + 124: b * 128 + 128, :], kvn[4 * b:4 * b + 4, 0:128], reads=['kvn'], is_output=True)
        P.dma('sp', dstv[b * 128 + 124: b * 128 + 128, :], kvn[4 * b:4 * b + 4, 128:256], reads=['kvn'], is_output=True)

    if stop == 'sa1':
        return finish()
    T8n_v = T8n[:].rearrange("p (hp par) q -> p hp par q", par=2)
    Pn_v = Pn[:].rearrange("p (hp par) q -> p hp par q", par=2)
    for h in range(8):
        kv, c, hf = h // 4, h // 2, h % 2
        sn_ps = bank_f[2 - hf]
        P.mm(sn_ps[:64, (h // 2) * 64:(h // 2 + 1) * 64], kT[hf * 64:(hf + 1) * 64, kv, 128:192], qT[hf * 64:(hf + 1) * 64, c, 0:64], True, True,
             ['kT', 'qT'], [BK(2 - hf)], sig=(h >= 6), skip=True)
    for par in range(2):
        tsn = tA[par][:64, 0:256].rearrange("p (hp q) -> p hp q", hp=4)
        P.tt('dve', tsn, bank_f[2 - par][:64, 0:256].rearrange("p (hp q) -> p hp q", hp=4), T8n_v[:, :, par, :], ALU.add,
             [BK(2 - par), 'T8n'], ['tA%d' % par])
        P.act(Pn_v[:, :, par, :], tsn, AF.Exp, ['tA%d' % par], ['Pn'], scale=0.125)
    if stop == 'sa1a':
        return finish()
    for i in range(3):
        P.ms('pool', cva[i][:, :, 64:65], 1.0, ['cva%d' % i])
    for i in range(2):
        P.ms('pool', Pcm[i][:], 0.0, ['Pcm%d' % i])
    for hf in range(2):
        P.mm(bank_f[6 + hf][:, :], zeros_bf[:, 0:128], zeros_bf[:, :], True, False, ['zeros_bf'], [BK(6 + hf)], sig=False, skip=True)
    if stop == 'sa1b':
        return finish()
    for h in range(8):
        kv = h // 4
        pv = bank_f[6 + h // 4][:64, (h % 4) * 128:(h % 4) * 128 + 65]
        P.mm(pv, Pn[:, h, :], Vaug[:64, 1, kv, :], False, False, ['Pn', 'Vaug'], [BK(6 + h // 4)], sig=False, skip=True)
    if stop == 'sa2':
        return finish()
    def satt_gen(b):
        i = b % 3
        yield
        j = b % 2
        yield
        P.dma('sp', ckb[i][:], ck.ap()[b], writes=['ckb%d' % i])
        yield
        P.dma('sp', cvb[i][:], cv.ap()[b], writes=['cvb%d' % i])
        yield
        P.cp('dve', ckbf[i][:], bc(ckb[i][:].rearrange("p (k d) -> p k d", k=2).unsqueeze(2), [128, 2, 2, 64]), ['ckb%d' % i], ['ckbf%d' % i])
        yield
        P.cp('pool', cva[i][:, :, 0:64], cvb[i][:].rearrange("p (k d) -> p k d", k=2), ['cvb%d' % i], ['cva%d' % i])
        yield
        tbk = 0 if j == 0 else 5
        yield
        sb0 = 3 if j == 0 else 1
        yield
        tSc = tA if j == 0 else tB
        yield
        tScN = 'tA' if j == 0 else 'tB'
        yield
        tp = bank_b[tbk].rearrange("p (k s) -> p k s", k=8)
        yield
        for kv in range(2):
            P.tr(tp[:, kv, :], ckbf[i][:, kv, :, :].rearrange("p r d -> p (r d)"), ident[:], ['ckbf%d' % i, 'ident'], [BK(tbk)])
        yield
        P.cp('act', kcT[i][:], tp[:, 0:2, :], [BK(tbk)], ['kcT%d' % i])
        yield
        T8c_v = T8[:, :, 128:132].rearrange("p (hp par) l -> p hp par l", par=2)
        yield
        for h in range(8):
            kv, c, hf = h // 4, h // 2, h % 2
            P.mm(bank_f[sb0 + hf][:, (h // 2) * 4:(h // 2 + 1) * 4], kcT[i][hf * 64:(hf + 1) * 64, kv, :], qT[hf * 64:(hf + 1) * 64, c, 4 * b:4 * b + 4], True, True,
                 ['kcT%d' % i, 'qT'], [BK(sb0 + hf)], sig=(h >= 6), skip=True)
        yield
        if b >= 2:
            P.ms('pool', Pcm[j][:, :, 4 * (b - 2):4 * (b - 2) + 4], 0.0, ['Pcm%d' % j])
        yield
        Pcm_v = Pcm[j][:].rearrange("p (hp par) q -> p hp par q", par=2)
        yield
        for par in range(2):
            tsc = tSc[2 + par][:, 0:16].rearrange("p (hp l) -> p hp l", hp=4)
            P.tt('dve', tsc, bank_f[sb0 + par][:, 0:16].rearrange("p (hp l) -> p hp l", hp=4), T8c_v[:, :, par, :], ALU.add,
                 [BK(sb0 + par), 'T8'], [tScN + '%d' % (2 + par)])
            P.act(Pcm_v[:, :, par, 4 * b:4 * b + 4], tsc, AF.Exp, [tScN + '%d' % (2 + par)], ['Pcm%d' % j], scale=0.125)
        yield
        for h in range(8):
            kv = h // 4
            pv = bank_f[6 + h // 4][:64, (h % 4) * 128:(h % 4) * 128 + 65]
            last = (b == 15 and h % 4 == 3)
            P.mm(pv, Pcm[j][:, h, :], cva[i][:, kv, :], False, last, ['Pcm%d' % j, 'cva%d' % i], [BK(6 + h // 4)], sig=last or (h == 7), skip=True)
        yield

    for b0 in range(0, 16, 2):
        interleave(satt_gen(b0), satt_gen(b0 + 1))
    if stop == 'sa3':
        return finish()
    attn_epilogue(64, SO)

    if stop == 'sattn':
        return finish()
    P.barrier()
    P.free('kvn', 'ckb0', 'ckb1', 'ckbf0', 'ckbf1', 'cvb0', 'cvb1', 'cvb2', 'cva0', 'cva1', 'cva2', 'kcT0', 'kcT1', 'kcT2', 'Pcm0', 'Pcm1', 'Pn', 'ckb2', 'ckbf2')
    S0 = [P.sbuf("S0_%d" % i, [128, 4, 128], F32) for i in range(3)]
    S0b = [P.sbuf("S0b%d" % i, [128, 4, 128], BF16) for i in range(3)]
    khm = [P.sbuf("khm%d" % i, [64, 4, 128], BF16) for i in range(3)]
    Am_s = P.sbuf("Am_s", [64, 4, 64], BF16)
    khT_s = P.sbuf("khT_s", [64, 4, 128], BF16)
    hgrn_elem(64, SO, True)
    A_ps = bank_f[5]; kT_ps = bank_b[6]; o_ps = bank_f[7]; U_ps = bank_f[4]
    for hh in range(4):
        P.mm(A_ps[:64, hh * 64:(hh + 1) * 64], kt[:, hh, 0:64], qt[:, hh, 0:64], True, True, ['kt', 'qt'], [BK(5)], sig=(hh == 3), skip=True)
    for hh in range(4):
        P.tr(kT_ps[:64, hh * 128:(hh + 1) * 128], kh[:, hh, 0:64], ident[:], ['kh', 'ident'], [BK(6)])
    P.tt('dve', Am_s[:], A_ps[:64, 0:256].rearrange("p (h t) -> p h t", h=4), bc(masks[:].unsqueeze(1), [64, 4, 64]), ALU.mult,
         [BK(5), 'masks'], ['Am_s'])
    P.cp('act', khT_s[:].rearrange("p h k -> p (h k)"), kT_ps[:64, 0:512], [BK(6)], ['khT_s'])
    P.mm(o_ps[:, :], zeros_bf[:, 0:128], zeros_bf[:, :], True, False, ['zeros_bf'], [BK(7)], sig=False, skip=True)
    for hh in range(4):
        P.mm(o_ps[:, hh * 64:(hh + 1) * 64], Vh[:64, 0, hh * 128:(hh + 1) * 128], Am_s[:, hh, :], False, False, ['Vh', 'Am_s'], [BK(7)],
             sig=False, skip=True)
    sh_v = sh.ap().rearrange("b h k v -> b k h v")
    nh_v = nh_s.ap().rearrange("b h k v -> b k h v")
    def shg_gen(b):
        i = b % 3
        yield
        ub = 4 if b % 2 == 0 else 3
        yield
        P.dma('sp', S0[i][:], sh_v[b], writes=['S0_%d' % i])
        yield
        P.cp('act', S0b[i][:], S0[i][:], ['S0_%d' % i], ['S0b%d' % i])
        yield
        P.act(khm[i][:].rearrange("p h k -> p (h k)"), khT_s[:].rearrange("p h k -> p (h k)"), AF.Identity, ['khT_s', 'rowm'], ['khm%d' % i],
              scale=rowm[:, b:b + 1])
        yield
        fill(0)
        yield
        for hh in range(4):
            last = (b == 15 and hh == 3)
            P.mm(o_ps[:, hh * 64 + 4 * b: hh * 64 + 4 * b + 4], S0b[i][:, hh, :], qt[:, hh, 4 * b:4 * b + 4], False, last,
                 ['S0b%d' % i, 'qt'], [BK(7)], sig=last or hh == 3, skip=True)
        yield
        for hh in range(4):
            P.mm(bank_f[ub][:, hh * 128:(hh + 1) * 128], khm[i][:, hh, :], Vh[:64, 0, hh * 128:(hh + 1) * 128], True, True, ['khm%d' % i, 'Vh'], [BK(ub)],
                 sig=(hh == 3), skip=True)
        yield
        P.tt('dve', S0[i][:], S0[i][:], bc(dch[:, :, b:b + 1], [128, 4, 128]), ALU.mult, ['S0_%d' % i, 'dch'], ['S0_%d' % i])
        yield
        P.tt('dve', S0[i][:], S0[i][:], bank_f[ub][:, :].rearrange("p (h v) -> p h v", h=4), ALU.add, ['S0_%d' % i, BK(ub)], ['S0_%d' % i])
        yield
        P.dma('pool', nh_v[b], S0[i][:], reads=['S0_%d' % i], is_output=True)
        yield

    for b0 in range(0, 16, 2):
        interleave(shg_gen(b0), shg_gen(b0 + 1))
    P.cp('act', o_loc[:, :, SO:SO + 64], o_ps[:, 0:256].rearrange("p (h t) -> p h t", h=4), [BK(7)], ['o_loc'])

    if stop == 'shg':
        return finish()
    P.barrier()
    P.free('w_bf', 'w_kd', 'hT', 'kT', 'Vaug', 'qT', 'T8', 'T8n', 'esink', 'lb', 'ln1mlb', 'maskbd', 'masks', 'scanm', 'rowm', 'o_att',
           'o_attn', 'Vh', 'kt', 'kh', 'qt', 'dch', 'cbl', 'S0_0', 'S0_1', 'S0_2', 'S0b0', 'S0b1', 'S0b2', 'khm0', 'khm1', 'khm2', 'Am_s', 'khT_s')
    ohT = P.sbuf("ohT", [128, 4, NTOK], BF16)
    w_ob = P.sbuf("w_ob", [128, 8, 1024], BF16)
    ccr = P.sbuf("ccr", [128, 4, 516], F32)
    Sst = P.sbuf("Sst", [128, 4, 128], F32)
    Sst_bf = P.sbuf("Sst_bf", [128, 4, 128], BF16)
    alpha = P.sbuf("alpha", [128, 4], F32)
    sqb = [P.sbuf("sqb%d" % i, [128, 512], BF16) for i in range(4)]
    P.dma('pool', w_ob[:], w_out.ap().rearrange("(c p) n -> p c n", p=128), writes=['w_ob'])
    P.dma('sp', ccr[:], cc1_out.ap().rearrange("(r p) f -> p r f", p=128), reads=['cc1_out'], writes=['ccr'])
    P.ms('dve', Sst[:], 0.0, ['Sst'])
    for j in range(4):
        P.ts('dve', alpha[:], ccr[:, j, 512:516], ACTF[j], NACT[j], ALU.mult, ALU.add, ['ccr', 'fl'], ['alpha'])
        P.tt('dve', Sst[:], Sst[:], bc(alpha[:].unsqueeze(2), [128, 4, 128]), ALU.mult, ['Sst', 'alpha'], ['Sst'])
        P.stt(Sst[:].rearrange("p h v -> p (h v)"), ccr[:, j, 0:512], ACTF[j], Sst[:].rearrange("p h v -> p (h v)"), ALU.mult, ALU.add,
              ['ccr', 'fl', 'Sst'], ['Sst'])
    P.cp('act', Sst_bf[:], Sst[:], ['Sst'], ['Sst_bf'])
    P.tt('dve', Sst[:], Sst[:], bc(cc1_sb[:, 512:516].unsqueeze(2), [128, 4, 128]), ALU.mult, ['Sst', 'cc1_sb', 'Sst_bf'], ['Sst'])
    P.tt('dve', Sst[:].rearrange("p h v -> p (h v)"), Sst[:].rearrange("p h v -> p (h v)"), cc1_sb[:, 0:512], ALU.add, ['Sst', 'cc1_sb'], ['Sst'])
    P.dma('sp', nh_p.ap().rearrange("h k v -> k h v"), Sst[:], reads=['Sst'], is_output=True)

    tiles512 = [(i * 512, 512) for i in range(4)] + [(SO, 64)]
    tset = [(tA[0], 'tA0', tA[1], 'tA1'), (tA[2], 'tA2', tA[3], 'tA3'), (tB[0], 'tB0', tB[1], 'tB1'), (tB[2], 'tB2', tB[3], 'tB3')]
    for (t0, n) in tiles512:
        H = range(4)
        ot = [tset[h][0][:, :n] for h in H]; otn = [tset[h][1] for h in H]
        rs = [tset[h][2][:, :n] for h in H]; rsn = [tset[h][3] for h in H]
        bm = [bank_f[2 * h] for h in H]; bs = [bank_f[2 * h + 1] for h in H]
        for h in H:
            if t0 < SO:
                P.mm(bm[h][:, :n], Sst_bf[:, h, :], qg[:, h, t0:t0 + n], True, True, ['Sst_bf', 'qg'], [BK(2 * h)])
        for h in H:
            if t0 < SO:
                P.tt('dve', ot[h], bm[h][:, :n], o_loc[:, h, t0:t0 + n], ALU.add, [BK(2 * h), 'o_loc'], [otn[h]])
            else:
                P.cp('dve', ot[h], o_loc[:, h, t0:t0 + n], ['o_loc'], [otn[h]])
        for h in H:
            P.act(sqb[h][:, :n], ot[h], AF.Square, [otn[h]], ['sqb%d' % h])
        for h in H:
            P.mm(bs[h][:, :n], ones_bf[:], sqb[h][:, :n], True, True, ['ones_bf', 'sqb%d' % h], [BK(2 * h + 1)])
        for h in H:
            P.act(rs[h], bs[h][:, :n], AF.Ln, [BK(2 * h + 1)], [rsn[h]], scale=1.0 / 128, bias=EPS)
            P.act(rs[h], rs[h], AF.Exp, [rsn[h]], [rsn[h]], scale=-0.5)
        for h in H:
            P.tt('dve', ot[h], ot[h], rs[h], ALU.mult, [otn[h], rsn[h]], [otn[h]])
            P.tt('dve', ohT[:, h, t0:t0 + n], ot[h], gate[:, h, t0:t0 + n], ALU.mult, [otn[h], 'gate'], ['ohT'])
    if stop == 'p2a':
        return finish()

    P.barrier()
    P.free('o_loc', 'qg', 'gate', 'S', 'Gtot', 'cc1_sb', 'ccr', 'Sst', 'Sst_bf', 'alpha', 'sqb0', 'sqb1', 'sqb2', 'sqb3', 'tA0', 'tA1', 'tA2', 'tA3', 'tA4', 'tB0', 'tB1', 'tB2', 'tB3', 'tB4')
    h2T = P.sbuf("h2T", [128, 8, NTOK], BF16)
    x1 = [P.sbuf("x1_%d" % i, [128, 1024], F32) for i in range(17)]
    tiles128 = [(i * 128, 128) for i in range(16)] + [(SO, 64)]
    def outproj(ti):
        t0, n = tiles128[ti]
        i = xcnt[0] % 2
        xcnt[0] += 1
        src = xp.ap()[t0:t0 + n, :] if t0 < SO else xs.ap()
        P.dma('sp', xb[i][:n, :], src, writes=['xb%d' % i])
        for hf in range(2):
            bk = 2 + hf + 2 * (ti % 2)
            ps = bank_f[bk]
            for c in range(8):
                lhs = oaT[:, c, t0:t0 + n] if c < 4 else ohT[:, c - 4, t0:t0 + n]
                P.mm(ps[:n, :], lhs, w_ob[:, c, hf * 512:(hf + 1) * 512], c == 0, c == 7, ['oaT', 'ohT', 'w_ob'], [BK(bk)], sig=(c == 7))
            P.tt('dve', x1[ti][:n, hf * 512:(hf + 1) * 512], ps[:n, :], xb[i][:n, hf * 512:(hf + 1) * 512], ALU.add,
                 [BK(bk), 'xb%d' % i], ['x1_%d' % ti])

    def norm2(ti):
        t0, n = tiles128[ti]
        rms_to_T(None, n, g2T, h2T, 'h2T', t0, xsrc_sb=(x1[ti][:n, :], 'x1_%d' % ti), jb=7)

    for ti in range(17):
        outproj(ti)
        if ti > 0:
            norm2(ti - 1)
    norm2(16)
    if stop == 'p2b':
        return finish()
    P.barrier()
    P.free('oaT', 'ohT', 'w_ob', 'xn')
    fg = P.sbuf("fg", [128, 1024], F32)
    P.dma('sp', fg[:], bcast_rows(fgv, 1024), writes=['fg'])
    uT = P.sbuf("uT", [128, 8, NTOK], BF16)
    wfo = P.sbuf("wfo", [128, 8, 1024], BF16)
    wfi = [P.sbuf("wfi%d" % i, [128, 2, 8, 128], BF16) for i in range(3)]
    a_buf = P.sbuf("a_buf", [128, 2050], F32)
    cT = [P.sbuf("cT%d" % i, [128, 512], F32) for i in range(4)]
    ccnt = [0]
    a_last = P.sbuf("a_last", [128, 2, 22], F32)
    a_first = P.sbuf("a_first", [128, 8, 2], F32)
    g_first = P.sbuf("g_first", [128, 8, 2], F32)
    prevT = P.sbuf("prevT", [128, 2, 8], F32)
    fx = [P.sbuf("fx%d" % i, [128, 8, 2], F32) for i in range(2)]
    ccr2 = P.sbuf("ccr2", [128, 4, 16], F32)
    cc2_sb = P.sbuf("cc2_sb", [128, 16], F32)
    zb = P.sbuf("zb", [128, 16, 6], F32)
    prev_s = P.sbuf("prev_s", [128, 8, 32], F32)
    a_smp = P.sbuf("a_smp", [128, 8, 16, 2], F32)
    sct = P.sbuf("sct", [64, 1024], F32)
    ncs = sct
    yt = P.sbuf("yt", [128, 1024], F32)
    P.ms('pool', a_buf[:, 0:2], 0.0, ['a_buf'])
    w_fi_v = w_fi.ap().rearrange("(c p) n -> p c n", p=128)
    w_fo_v = w_fo.ap().rearrange("(c p) n -> p c n", p=128)
    wcnt = [0]
    f1cnt = [0]
    f2cnt = [0]

    for pi, (c_lo, c_hi) in enumerate(PARTS):
        ncp = c_hi - c_lo
        last_part = (pi == len(PARTS) - 1)
        P.dma('sp', sct[:32, 0:ncp * 128], scv.ap()[:, c_lo * 128:c_hi * 128], writes=['sct'])
        for cl in range(ncp):
            P.op('pe', lambda e, cl=cl: e.transpose(out=bank_f[0][:, cl * 32:(cl + 1) * 32], in_=sct[:32, cl * 128:(cl + 1) * 128],
                                                    identity=ident_f[:32, :32]), ['sct', 'ident_f'], [BK(0)])
        P.cp('act', prev_s[:, 0:ncp, :], bank_f[0][:, 0:ncp * 32].rearrange("p (c q) -> p c q", c=ncp), [BK(0)], ['prev_s'])
        for cl in range(ncp):
            c = c_lo + cl
            wi = c % 3
            wn = 'wfi%d' % wi
            for cn in ([0, 1] if c == 0 else []) + ([c + 2] if c + 2 < 22 else []):
                P.dma('pool', wfi[cn % 3][:, 0, :, :], w_fi_v[:, :, cn * 128:(cn + 1) * 128], writes=['wfi%d' % (cn % 3)])
                P.dma('pool', wfi[cn % 3][:, 1, :, :], w_fi_v[:, :, 2816 + cn * 128: 2816 + (cn + 1) * 128], writes=['wfi%d' % (cn % 3)])
            if cl == 0:
                P.dma('pool', wfo[:, 0:ncp, :], w_fo_v[:, c_lo:c_hi, :], writes=['wfo'])
            w0, w1, w2 = cwT[0][:, c:c + 1], cwT[1][:, c:c + 1], cwT[2][:, c:c + 1]
            bcol = cbT[:, c:c + 1]
            def tile_gen(t0, n):
                pset = 2 + 2 * (f1cnt[0] % 3)
                f1cnt[0] += 1
                pa = bank_f[pset]; pg = bank_f[pset + 1]
                pan = BK(pset); pgn = BK(pset + 1)
                for kc in range(8):
                    P.mm(pa[:, :n], wfi[wi][:, 0, kc, :], h2T[:, kc, t0:t0 + n], kc == 0, kc == 7, [wn, 'h2T'], [pan], sig=(kc == 7))
                yield
                for kc in range(8):
                    P.mm(pg[:, :n], wfi[wi][:, 1, kc, :], h2T[:, kc, t0:t0 + n], kc == 0, kc == 7, [wn, 'h2T'], [pgn], sig=(kc == 7))
                yield
                ci = ccnt[0] % 2
                ccnt[0] += 1
                t1 = cT[ci][:, :n]; sl = cT[2 + ci][:, :n]
                t1n = 'cT%d' % ci; sln = 'cT%d' % (2 + ci)
                if t0 < SO:
                    abn = 'a_buf%d' % (t0 // 512); abp = 'a_buf%d' % (t0 // 512 - 1) if t0 > 0 else 'a_buf'
                    P.cp('act', a_buf[:, 2 + t0:2 + t0 + n], pa[:, :n], [pan], [abn])
                    yield
                    P.act(t1, a_buf[:, t0:t0 + n], AF.Identity, [abn, abp, 'vec'], [t1n], scale=w0, bias=bcol)
                    yield
                    P.stt(t1, a_buf[:, t0 + 1:t0 + 1 + n], w1, t1, ALU.mult, ALU.add, [abn, abp, 'vec', t1n], [t1n])
                    yield
                    P.stt(t1, pa[:, :n], w2, t1, ALU.mult, ALU.add, [pan, 'vec', t1n], [t1n])
                    yield
                    P.act(sl, t1, AF.Silu, [t1n], [sln])
                    yield
                    P.tt('dve', uT[:, cl, t0:t0 + n], sl, pg[:, :n], ALU.mult, [sln, pgn], ['uT'])
                    yield
                    if t0 == 0:
                        P.cp('dve', a_first[:, cl, :], a_buf[:, 2:4], ['a_buf0'], ['a_first'])
                        P.cp('act', g_first[:, cl, :], pg[:, 0:2], [pgn], ['g_first'])
                    if t0 == 1536:
                        P.cp('dve', a_last[:, :, c], a_buf[:, 2048:2050], ['a_buf3'], ['a_last'])
                else:
                    P.cp('dve', zb[:, :, 0:2], prev_s[:, cl, :].rearrange("p (b j) -> p b j", j=2), ['prev_s'], ['zb'])
                    P.cp('act', zb[:, :, 2:6], pa[:, :64].rearrange("p (b l) -> p b l", l=4), [pan], ['zb'])
                    t3 = t1.rearrange("p (b l) -> p b l", l=4)
                    P.ts('dve', t3, zb[:, :, 0:4], w0, bcol, ALU.mult, ALU.add, ['zb', 'vec'], [t1n])
                    P.stt(t3, zb[:, :, 1:5], w1, t3, ALU.mult, ALU.add, ['zb', 'vec', t1n], [t1n])
                    P.stt(t3, zb[:, :, 2:6], w2, t3, ALU.mult, ALU.add, ['zb', 'vec', t1n], [t1n])
                    P.act(sl, t1, AF.Silu, [t1n], [sln])
                    P.tt('dve', uT[:, cl, t0:t0 + n], sl, pg[:, :n], ALU.mult, [sln, pgn], ['uT'])
                    P.cp('dve', a_smp[:, cl, :, :], zb[:, :, 4:6], ['zb'], ['a_smp'])

            interleave(tile_gen(*tiles512[0]), tile_gen(*tiles512[1]))
            interleave(tile_gen(*tiles512[2]), tile_gen(*tiles512[3]))
            interleave(tile_gen(*tiles512[4]))
        P.cp('dve', cc2_sb[:].rearrange("p (j c) -> p j c", j=2)[:, :, 0:ncp], a_last[:, :, c_lo:c_hi], ['a_last'], ['cc2_sb'])
        P.dma('pool', cc2_in[pi].ap(), cc2_sb[:], reads=['cc2_sb'], writes=['cc2_in%d' % pi])
        P.collective(cc2_in[pi], cc2_out[pi], GROUPS, ['cc2_in%d' % pi], ['cc2_out%d' % pi])
        P.dma('sp', ccr2[:], cc2_out[pi].ap().rearrange("(r p) f -> p r f", p=128), reads=['cc2_out%d' % pi], writes=['ccr2'])
        for cl in range(ncp):
            P.op('pe', lambda e, cl=cl: e.transpose(out=bank_f[0][:32, cl * 128:(cl + 1) * 128] if cl < 4 else bank_f[1][:32, (cl - 4) * 128:(cl - 3) * 128],
                                                    in_=a_smp[:, cl, :, :].rearrange("p b j -> p (b j)"), identity=ident_f[:, :]),
                 ['a_smp', 'ident_f'], [BK(0) if cl < 4 else BK(1)])
        P.cp('act', ncs[:32, 0:min(ncp, 4) * 128], bank_f[0][:32, 0:min(ncp, 4) * 128], [BK(0)], ['sct'])
        if ncp > 4:
            P.cp('act', ncs[:32, 512:ncp * 128], bank_f[1][:32, 0:(ncp - 4) * 128], [BK(1)], ['sct'])
        P.dma('sp', nc_s.ap()[:, c_lo * 128:c_hi * 128], ncs[:32, 0:ncp * 128], reads=['sct'], is_output=True)
        def ffn_out_tile(ti):
            t0, n = tiles128[ti]
            fset = 2 * (f2cnt[0] % 2)
            f2cnt[0] += 1
            for hf in range(2):
                ps = bank_f[2 + hf + fset]
                for cl in range(ncp):
                    P.mm(ps[:n, :], uT[:, cl, t0:t0 + n], wfo[:, cl, hf * 512:(hf + 1) * 512], cl == 0, cl == ncp - 1, ['uT', 'wfo'],
                         [BK(2 + hf + fset)], sig=(cl == ncp - 1))
                if not last_part:
                    P.tt('dve', x1[ti][:n, hf * 512:(hf + 1) * 512], ps[:n, :], x1[ti][:n, hf * 512:(hf + 1) * 512], ALU.add,
                         [BK(2 + hf + fset), 'x1_%d' % ti], ['x1_%d' % ti])
                else:
                    P.tt('dve', yt[:n, hf * 512:(hf + 1) * 512], ps[:n, :], x1[ti][:n, hf * 512:(hf + 1) * 512], ALU.add,
                         [BK(2 + hf + fset), 'x1_%d' % ti], ['yt'])
            if last_part:
                xo = xb[ti % 2]; xon = 'xb%d' % (ti % 2)
                P.act(xo[:n, :], yt[:n, :], AF.Square, ['yt'], [xon, 'stat'], accum=stat[:n, 0:1])
                P.act(stat[:n, 1:2], stat[:n, 0:1], AF.Ln, ['stat'], ['stat'], scale=1.0 / 1024, bias=EPS)
                P.act(stat[:n, 2:3], stat[:n, 1:2], AF.Exp, ['stat'], ['stat'], scale=-0.5)
                P.stt(xo[:n, :], yt[:n, :], stat[:n, 2:3], fg[:n, :], ALU.mult, ALU.mult, ['yt', 'stat', 'fg'], [xon])
                dst = y_p.ap()[t0:t0 + n, :] if t0 < SO else y_s.ap()
                P.dma('sp', dst, xo[:n, :], reads=[xon], is_output=True)

        for ti in range(1, 17):
            ffn_out_tile(ti)
        pTf = prevT[:].rearrange("p j c -> p (j c)")
        P.ts('dve', pTf, ccr2[:, 0, :], SEL[0], None, ALU.mult, ALU.bypass, ['ccr2', 'fl'], ['prevT'])
        for j in range(1, 4):
            P.stt(pTf, ccr2[:, j, :], SEL[j], pTf, ALU.mult, ALU.add, ['ccr2', 'fl', 'prevT'], ['prevT'])
        W = [cwT[j][:, c_lo:c_hi] for j in range(3)]
        bcs = cbT[:, c_lo:c_hi]
        f0 = fx[0][:, 0:ncp, :]; f1 = fx[1][:, 0:ncp, :]
        P.tt('dve', f0[:, :, 0], prevT[:, 0, 0:ncp], W[0], ALU.mult, ['prevT', 'vec'], ['fx0'])
        P.tt('dve', f0[:, :, 1], prevT[:, 1, 0:ncp], W[0], ALU.mult, ['prevT', 'vec', 'fx0'], ['fx0'])
        P.tt('dve', f1[:, :, 0], prevT[:, 1, 0:ncp], W[1], ALU.mult, ['prevT', 'vec'], ['fx1'])
        P.tt('dve', f1[:, :, 1], a_first[:, 0:ncp, 0], W[1], ALU.mult, ['a_first', 'vec', 'fx1'], ['fx1'])
        P.tt('dve', f0, f0, f1, ALU.add, ['fx0', 'fx1'], ['fx0'])
        P.tt('dve', f1[:, :, 0], a_first[:, 0:ncp, 0], W[2], ALU.mult, ['a_first', 'vec', 'fx1'], ['fx1'])
        P.tt('dve', f1[:, :, 1], a_first[:, 0:ncp, 1], W[2], ALU.mult, ['a_first', 'vec', 'fx1'], ['fx1'])
        P.tt('dve', f0, f0, f1, ALU.add, ['fx0', 'fx1'], ['fx0'])
        P.tt('dve', f0, f0, bc(bcs.unsqueeze(2), [128, ncp, 2]), ALU.add, ['fx0', 'vec'], ['fx0'])
        P.act(f1, f0, AF.Silu, ['fx0', 'fx1'], ['fx1'])
        P.tt('dve', uT[:, 0:ncp, 0:2], f1, g_first[:, 0:ncp, :], ALU.mult, ['fx1', 'g_first'], ['uT'])
        ffn_out_tile(0)

    P.op('pe', lambda e: e.transpose(out=bank_f[0][:44, 0:128], in_=a_last[:].rearrange("p j c -> p (j c)"), identity=ident_f[:, :]),
         ['a_last', 'ident_f'], [BK(0)])
    P.cp('act', ncs[:44, 0:128], bank_f[0][:44, 0:128], [BK(0)], ['sct'])
    for j in range(2):
        P.dma('sp', nc_p.ap()[j].rearrange("(c p) -> c p", p=128), ncs[j * 22:(j + 1) * 22, 0:128], reads=['sct'], is_output=True)

    P.emit()
    P.close()
    return nc


def _t5_bucket(dist):
    import math
    n = np.maximum(dist, 0)
    max_exact = 16
    nf = np.maximum(n, 1).astype(np.float32)
    large = max_exact + (np.log(nf / max_exact) / math.log(128 / max_exact) * (32 - max_exact)).astype(np.int32)
    large = np.minimum(large, 31)
    return np.where(n < max_exact, n, large)


def _constants():
    c = {}
    c["c_ident"] = np.eye(128, dtype=np.float32)
    oh = np.zeros((32, 128), np.float32)
    bk = _t5_bucket(np.arange(128))
    oh[bk, np.arange(128)] = 1.0
    c["c_onehot"] = oh
    s = np.arange(128)[:, None]; t = np.arange(128)[None, :]
    c["c_maskbd"] = ((s // 64 == t // 64) & (s <= t)).astype(np.float32)
    s = np.arange(64)[:, None]; t = np.arange(64)[None, :]
    c["c_masks"] = ((s // 4 == t // 4) & (s <= t)).astype(np.float32)
    c["c_bdneg"] = np.where(s // 4 == t // 4, 0.0, NEG * 8.0).astype(np.float32)
    m = np.ones((1, 512), np.float32); m[0, ::64] = 0.0
    c["c_scanm"] = m
    c["c_rowm"] = (np.arange(64)[:, None] // 4 == np.arange(16)[None, :]).astype(np.float32)
    return c


_CACHE = {}


def kernel(x_prompt, x_sample, cache_k_win, cache_v_win, state_hgrn, state_conv,
           norm1_g, w_in, attn_sinks, rel_bias, lb_gamma, attn_out_g, hg_out_g, w_out,
           norm2_g, w_ffn_in, conv_w, conv_b, w_ffn_out, final_g):
    f = lambda a: np.ascontiguousarray(np.asarray(a, dtype=np.float32))
    x_prompt, x_sample = f(x_prompt), f(x_sample)
    ckw, cvw, sth, stc = f(cache_k_win)[0], f(cache_v_win)[0], f(state_hgrn)[0], f(state_conv)[0]
    fm = lambda v, n: f(v).reshape(n, 128).T
    vecs = np.concatenate([fm(norm1_g[0], 8), fm(norm2_g[0], 8), fm(attn_out_g[0], 4), fm(hg_out_g[0], 4),
                           fm(lb_gamma[0], 4), fm(lb_gamma[1], 4),
                           fm(conv_w[0][0], 22), fm(conv_w[0][1], 22), fm(conv_w[0][2], 22), fm(conv_b[0], 22)], axis=1)
    vecs = np.ascontiguousarray(vecs, dtype=np.float32)
    shared = {
        "w_in": f(w_in)[0], "w_out": f(w_out)[0], "w_fi": f(w_ffn_in)[0], "w_fo": f(w_ffn_out)[0],
        "vecs": vecs, "fgv": f(final_g).reshape(1, 1024), "sinks": f(attn_sinks).reshape(1, 8), "relb": f(rel_bias),
    }
    shared.update(_constants())
    in_maps = []
    for c in range(8):
        b, r = c // 4, c % 4
        fl = np.zeros((1, 16), np.float32)
        fl[0, 0] = 0.0 if r > 0 else NEG * 8.0
        for j in range(4):
            fl[0, 1 + j] = 1.0 if j < r else 0.0
            fl[0, 5 + j] = 0.0 if j < r else 1.0
            fl[0, 9 + j] = 1.0 if j == r - 1 else 0.0
        m = dict(shared)
        m["xp"] = x_prompt[b, r * 2048:(r + 1) * 2048]
        m["xh"] = x_prompt[b, r * 2048 - 128:r * 2048] if r > 0 else np.zeros((128, 1024), np.float32)
        m["xs"] = x_sample[16 * c:16 * c + 16].reshape(64, 1024)
        m["ck"] = ckw[16 * c:16 * c + 16].reshape(16, 128, 128)
        m["cv"] = cvw[16 * c:16 * c + 16].reshape(16, 128, 128)
        m["sh"] = sth[16 * c:16 * c + 16]
        m["scv"] = stc[16 * c:16 * c + 16].reshape(32, 2816)
        m["flags"] = fl
        in_maps.append({k: np.ascontiguousarray(v) for k, v in m.items()})
    if "nc" not in _CACHE:
        _CACHE["nc"] = build_program()
    res = run_bass_kernel_spmd(_CACHE["nc"], in_maps, core_ids=list(range(8)))
    R = res.results
    y_prompt = np.stack([np.concatenate([R[4 * b + r]["y_p"] for r in range(4)], 0) for b in range(2)], 0)
    y_sample = np.concatenate([R[c]["y_s"].reshape(16, 4, 1024) for c in range(8)], 0)
    nk_p = np.stack([R[4 * b + 3]["nk_p"].reshape(128, 2, 64) for b in range(2)], 0)[None]
    nv_p = np.stack([R[4 * b + 3]["nv_p"].reshape(128, 2, 64) for b in range(2)], 0)[None]
    nh_p = np.stack([R[4 * b + 3]["nh_p"] for b in range(2)], 0)[None]
    nc_p = np.stack([R[4 * b + 3]["nc_p"] for b in range(2)], 0)[None]
    nk_s = np.concatenate([R[c]["nk_s"].reshape(16, 128, 2, 64) for c in range(8)], 0)[None]
    nv_s = np.concatenate([R[c]["nv_s"].reshape(16, 128, 2, 64) for c in range(8)], 0)[None]
    nh_s = np.concatenate([R[c]["nh_s"] for c in range(8)], 0)[None]
    nc_s = np.concatenate([R[c]["nc_s"].reshape(16, 2, 2816) for c in range(8)], 0)[None]
    outs = (y_prompt, y_sample, nk_p, nv_p, nh_p, nc_p, nk_s, nv_s, nh_s, nc_s)
    return tuple(np.ascontiguousarray(o, dtype=np.float32) for o in outs)
```

```python
import os
import numpy as np
import concourse.bass as bass
import concourse.mybir as mybir
from concourse.bass_utils import run_bass_kernel_spmd

F32 = mybir.dt.float32
BF16 = mybir.dt.bfloat16
ALU = mybir.AluOpType
AF = mybir.ActivationFunctionType

NEG = -30000.0
EPS = 1e-6
NTOK = 2048 + 64
H0 = 12


class Prog:
    ENG = ('pe', 'act', 'dve', 'pool', 'sp')

    def __init__(self, nc, n_dma_sems=16):
        self.nc = nc
        self.stack = []
        self.ops = {e: [] for e in self.ENG}
        self.cnt = {e: 0 for e in self.ENG}
        self.sem = {}
        for e in ('pe', 'act', 'dve', 'pool'):
            self.sem[e] = self._sem('s_' + e)
        self.dma_sems = [self._sem('s_dma%d' % i) for i in range(n_dma_sems + 8)]
        self.dma_cnt = [0] * (n_dma_sems + 8)
        self.dma_pool_ids = {'sp': list(range(n_dma_sems)), 'pool': list(range(n_dma_sems, n_dma_sems + 8))}
        self.dma_rr = {'sp': 0, 'pool': 0}
        self.waited = {e: {} for e in self.ENG}
        self.last_w = {}
        self.readers = {}
        self.extra_waits = {e: [] for e in self.ENG}
        self.out_tokens = []
        self.cc_sems = []

    def _sem(self, name):
        g = self.nc.semaphore(name)
        s = g.__enter__()
        self.stack.append(g)
        return s

    def push(self):
        return len(self.stack)

    def pop_to(self, mark):
        while len(self.stack) > mark:
            self.stack.pop().__exit__(None, None, None)

    def arena_init(self, words):
        g = self.nc.sbuf_tensor("arena", [128, words], F32)
        self.arena = g.__enter__()
        self.stack.append(g)
        self.free_list = [(0, words)]
        self.allocs = {}
        self.arena_words = words

    def sbuf(self, name, shape, dtype):
        n = 1
        for d in shape[1:]:
            n *= d
        words = n if dtype == F32 else (n + 1) // 2
        words = (words + 7) // 8 * 8
        for i, (off, sz) in enumerate(self.free_list):
            if sz >= words:
                break
        else:
            raise AssertionError(("SBUF arena overflow", name, words, self.free_list))
        if sz == words:
            self.free_list.pop(i)
        else:
            self.free_list[i] = (off + words, sz - words)
        assert name not in self.allocs, name
        self.allocs[name] = (off, words)
        v = self.arena[:shape[0], off:off + words]
        if dtype != F32:
            v = v.bitcast(dtype)
        v = v[:, 0:n]
        if len(shape) == 3:
            v = v.rearrange("p (a b) -> p a b", a=shape[1])
        elif len(shape) == 4:
            v = v.rearrange("p (a b c) -> p a b c", a=shape[1], b=shape[2])
        return v

    def free(self, *names):
        for name in names:
            self.free_list.append(self.allocs.pop(name))
        self.free_list.sort()
        merged = []
        for off, sz in self.free_list:
            if merged and merged[-1][0] + merged[-1][1] == off:
                merged[-1] = (merged[-1][0], merged[-1][1] + sz)
            else:
                merged.append((off, sz))
        self.free_list = merged

    def used(self):
        return self.arena_words - sum(sz for _, sz in self.free_list)

    def psum(self, name, shape, dtype):
        g = self.nc.psum_tensor(name, list(shape), dtype)
        t = g.__enter__()
        self.stack.append(g)
        return t

    def _deps(self, eng, reads, writes):
        toks = []
        for b in reads:
            t = self.last_w.get(b)
            if t is not None:
                toks.append(t)
        for b in writes:
            t = self.last_w.get(b)
            if t is not None:
                toks.append(t)
            toks.extend(self.readers.get(b, ()))
        waits = {}
        for sem, val in self.extra_waits[eng]:
            toks.append((sem, val, None, 'x'))
        self.extra_waits[eng] = []
        for sem, val, teng, kind in toks:
            if teng == eng and kind == 'c' and eng == 'pe':
                continue
            key = id(sem)
            if self.waited[eng].get(key, 0) >= val:
                continue
            if key not in waits or waits[key][1] < val:
                waits[key] = (sem, val)
        for key, (sem, val) in waits.items():
            self.waited[eng][key] = val
        return list(waits.values())

    def _commit(self, tok, reads, writes):
        for b in reads:
            self.readers.setdefault(b, []).append(tok)
        for b in writes:
            self.last_w[b] = tok
            self.readers[b] = []

    @staticmethod
    def _excl(reads, writes):
        bk = [b for b in reads if b.startswith('bk')]
        if not bk:
            return reads, writes
        return [b for b in reads if not b.startswith('bk')], list(writes) + bk

    def op(self, eng, fn, reads=(), writes=(), sig=True):
        reads, writes = self._excl(reads, writes)
        waits = self._deps(eng, reads, writes)
        if sig:
            self.cnt[eng] += 1
            tok = (self.sem[eng], self.cnt[eng], eng, 'c')
        else:
            assert eng == 'pe'
            tok = (self.sem[eng], self.cnt[eng] + 1, eng, 'c')
        self.ops[eng].append((waits, fn, (self.sem[eng], 1) if sig else None))
        self._commit(tok, reads, writes)
        return tok

    def dma(self, eng, out, in_, reads=(), writes=(), is_output=False, **kw):
        ids = self.dma_pool_ids[eng]
        i = ids[self.dma_rr[eng] % len(ids)]
        self.dma_rr[eng] += 1
        sem = self.dma_sems[i]
        waits = self._deps(eng, reads, writes)
        prev = self.dma_cnt[i]
        if prev > 0 and self.waited[eng].get(id(sem), 0) < prev:
            waits.append((sem, prev))
            self.waited[eng][id(sem)] = prev
        self.dma_cnt[i] += 16
        tok = (sem, self.dma_cnt[i], eng, 'd')
        self.ops[eng].append((waits, lambda e: e.dma_start(out=out, in_=in_, **kw), (sem, 16)))
        self._commit(tok, reads, writes)
        if is_output:
            self.out_tokens.append(tok)
        return tok

    def collective(self, cin, cout, groups, reads, writes):
        sem = self._sem('s_cc%d' % len(self.cc_sems))
        self.cc_sems.append(sem)
        waits = self._deps('pool', reads, writes)
        tok = (sem, 1, 'pool', 'd')
        self.ops['pool'].append((waits, lambda e: e.collective_compute(
            "AllGather", ALU.bypass, replica_groups=groups, ins=[cin.ap().opt()], outs=[cout.ap().opt()]), (sem, 1)))
        self._commit(tok, reads, writes)
        return tok

    def barrier(self, skip_pool_dma=False):
        toks = []
        for e in ('pe', 'act', 'dve', 'pool'):
            if self.cnt[e] > 0:
                toks.append((self.sem[e], self.cnt[e]))
        for i, s in enumerate(self.dma_sems):
            if skip_pool_dma and i in self.dma_pool_ids['pool']:
                continue
            if self.dma_cnt[i] > 0:
                toks.append((s, self.dma_cnt[i]))
        for e in self.ENG:
            self.extra_waits[e].extend(toks)

    def emit(self):
        nc = self.nc
        fin = {}
        for sem, val, _, _ in self.out_tokens:
            if id(sem) not in fin or fin[id(sem)][1] < val:
                fin[id(sem)] = (sem, val)
        final_waits = list(fin.values())
        engobj = {'pe': 'tensor', 'act': 'scalar', 'dve': 'vector', 'pool': 'gpsimd', 'sp': 'sync'}
        with nc.Block() as block:
            for e in self.ENG:
                def body(engine, ops=self.ops[e], fw=(final_waits if e == 'sp' else [])):
                    for waits, fn, inc in ops:
                        for sem, val in waits:
                            engine.wait_ge(sem, val)
                        ins = fn(engine)
                        if inc is not None:
                            ins.then_inc(inc[0], inc[1])
                    for sem, val in fw:
                        engine.wait_ge(sem, val)
                getattr(block, engobj[e])(body)

    def close(self):
        self.pop_to(0)

    def mm(self, out, lhsT, rhs, start, stop, r, w, sig=True, skip=False):
        bp = lhsT.base_partition()
        if bp != 0:
            return self.op('pe', lambda e: e.matmul(out, lhsT=lhsT, rhs=rhs, start=start, stop=stop,
                                                    skip_group_check=skip, tile_position=(bp, 0)), r, w, sig)
        return self.op('pe', lambda e: e.matmul(out, lhsT=lhsT, rhs=rhs, start=start, stop=stop,
                                                skip_group_check=skip), r, w, sig)

    def tr(self, out, in_, ident, r, w):
        return self.op('pe', lambda e: e.transpose(out=out, in_=in_, identity=ident), r, w)

    def act(self, out, in_, func, r, w, scale=1.0, bias=0.0, accum=None):
        if accum is None:
            return self.op('act', lambda e: e.activation(out=out, in_=in_, func=func, scale=scale, bias=bias), r, w)
        return self.op('act', lambda e: e.activation(out=out, in_=in_, func=func, scale=scale, bias=bias,
                                                     accum_out=accum), r, w)

    def tt(self, eng, out, in0, in1, op, r, w):
        return self.op(eng, lambda e: e.tensor_tensor(out=out, in0=in0, in1=in1, op=op), r, w)

    def ts(self, eng, out, in0, s1, s2, op0, op1, r, w):
        return self.op(eng, lambda e: e.tensor_scalar(out=out, in0=in0, scalar1=s1, scalar2=s2, op0=op0, op1=op1), r, w)

    def stt(self, out, in0, scalar, in1, op0, op1, r, w):
        return self.op('dve', lambda e: e.scalar_tensor_tensor(out=out, in0=in0, scalar=scalar, in1=in1,
                                                               op0=op0, op1=op1), r, w)

    def cp(self, eng, out, in_, r, w):
        if eng == 'act':
            return self.op('act', lambda e: e.copy(out=out, in_=in_), r, w)
        return self.op(eng, lambda e: e.tensor_copy(out=out, in_=in_), r, w)

    def ms(self, eng, ap, val, w):
        return self.op(eng, lambda e: e.memset(ap, val), (), w)

    def recip(self, out, in_, r, w):
        return self.op('dve', lambda e: e.reciprocal(out=out, in_=in_), r, w)

    def scan(self, out, d0, d1, init, r, w):
        return self.op('dve', lambda e: e.tensor_tensor_scan(out=out, data0=d0, data1=d1, initial=init,
                                                             op0=ALU.mult, op1=ALU.add), r, w)


def bc(ap, shape):
    return ap.to_broadcast(list(shape))


def dup_cols(ap2):
    a = ap2.ap
    return bass.AP(ap2.tensor, ap2.offset, [list(a[0]), [0, 2], list(a[-1])])


PARTS = [(0, 8), (8, 16), (16, 22)]


def build_program(stop=None):
    nc = bass.Bass("TRN2", target_bir_lowering=False)

    def din(name, shape):
        return nc.dram_tensor(name, list(shape), F32, kind="ExternalInput")

    def dout(name, shape):
        return nc.dram_tensor(name, list(shape), F32, kind="ExternalOutput")

    xp = din("xp", [2048, 1024]); xh = din("xh", [128, 1024]); xs = din("xs", [64, 1024])
    ck = din("ck", [16, 128, 128]); cv = din("cv", [16, 128, 128])
    sh = din("sh", [16, 4, 128, 128]); scv = din("scv", [32, 2816])
    w_in = din("w_in", [1024, 2816]); w_out = din("w_out", [1024, 1024])
    w_fi = din("w_fi", [1024, 5632]); w_fo = din("w_fo", [2816, 1024])
    vecs = din("vecs", [128, 120]); fgv = din("fgv", [1, 1024]); sinks = din("sinks", [1, 8])
    relb = din("relb", [32, 8]); flags = din("flags", [1, 16])
    c_ident = din("c_ident", [128, 128]); c_onehot = din("c_onehot", [32, 128])
    c_maskbd = din("c_maskbd", [128, 128]); c_masks = din("c_masks", [64, 64])
    c_bdneg = din("c_bdneg", [64, 64]); c_scanm = din("c_scanm", [1, 512]); c_rowm = din("c_rowm", [64, 16])

    y_p = dout("y_p", [2048, 1024]); y_s = dout("y_s", [64, 1024])
    nk_p = dout("nk_p", [128, 128]); nv_p = dout("nv_p", [128, 128])
    nh_p = dout("nh_p", [4, 128, 128]); nc_p = dout("nc_p", [2, 2816])
    nk_s = dout("nk_s", [16, 128, 128]); nv_s = dout("nv_s", [16, 128, 128])
    nh_s = dout("nh_s", [16, 4, 128, 128]); nc_s = dout("nc_s", [32, 2816])

    scr = nc.dram_tensor("scr_toep", [8 * 128 * 383], F32, kind="Internal")
    cc1_in = nc.dram_tensor("cc1_in", [128, 516], F32, kind="Internal")
    cc1_out = nc.dram_tensor("cc1_out", [4 * 128, 516], F32, kind="Internal")
    cc2_in = [nc.dram_tensor("cc2_in%d" % i, [128, 16], F32, kind="Internal") for i in range(3)]
    cc2_out = [nc.dram_tensor("cc2_out%d" % i, [4 * 128, 16], F32, kind="Internal") for i in range(3)]
    GROUPS = [[0, 1, 2, 3], [4, 5, 6, 7]]

    P = Prog(nc)
    P.arena_init(53000)

    def finish():
        P.emit()
        P.close()
        return nc

    bank_f = [P.psum("bk%d" % i, [128, 512], F32) for i in range(8)]
    bank_b = [b[:].bitcast(BF16) for b in bank_f]
    BK = lambda i: 'bk%d' % i

    ident = P.sbuf("ident", [128, 128], BF16)
    ident_f = P.sbuf("ident_f", [128, 128], F32)
    ones_bf = P.sbuf("ones_bf", [128, 128], BF16)
    zeros_bf = P.sbuf("zeros_bf", [128, 512], BF16)
    vec = P.sbuf("vec", [128, 120], F32)
    fl = P.sbuf("fl", [128, 16], F32)
    stat = P.sbuf("stat", [128, 8], F32)
    onesf8 = P.sbuf("onesf8", [128, 16], F32)
    g1T = vec[:, 0:8]; g2T = vec[:, 8:16]; aogT = vec[:, 16:20]; hggT = vec[:, 20:24]
    lbg0 = vec[:, 24:28]; lbg1 = vec[:, 28:32]
    cwT = [vec[:, 32 + 22 * j: 32 + 22 * (j + 1)] for j in range(3)]; cbT = vec[:, 98:120]
    HALO = fl[:, 0:1]
    ACTF = [fl[:, 1 + j: 2 + j] for j in range(4)]
    NACT = [fl[:, 5 + j: 6 + j] for j in range(4)]
    SEL = [fl[:, 9 + j: 10 + j] for j in range(4)]

    oaT = P.sbuf("oaT", [128, 4, NTOK], BF16)
    o_loc = P.sbuf("o_loc", [128, 4, NTOK], BF16)
    qg = P.sbuf("qg", [128, 4, NTOK], BF16)
    gate = P.sbuf("gate", [128, 4, NTOK], BF16)
    S = P.sbuf("S", [128, 4, 128], F32)
    Gtot = P.sbuf("Gtot", [128, 4], F32)
    cc1_sb = P.sbuf("cc1_sb", [128, 516], F32)

    w_bf = P.sbuf("w_bf", [128, 8, 2816], BF16)
    hT = P.sbuf("hT", [128, 8, 512], BF16)
    kT = P.sbuf("kT", [128, 2, 5 * 128], BF16)
    Vaug = P.sbuf("Vaug", [128, 5, 2, 65], BF16)
    qT = P.sbuf("qT", [128, 4, 512], BF16)
    xb = [P.sbuf("xb%d" % i, [128, 1024], F32) for i in range(2)]
    xn = P.sbuf("xn", [128, 1024], BF16)
    T8 = P.sbuf("T8", [128, 8, 256], BF16)
    T8x = P.sbuf("T8x", [128, 8, 256], BF16)
    T8n = P.sbuf("T8n", [64, 8, 64], BF16)
    esink = P.sbuf("esink", [128, 8], F32)
    lb = P.sbuf("lb", [128, 4], F32)
    ln1mlb = P.sbuf("ln1mlb", [128, 4], F32)
    maskbd = P.sbuf("maskbd", [128, 128], F32)
    masks = P.sbuf("masks", [64, 64], F32)
    scanm = P.sbuf("scanm", [128, 512], BF16)
    scanm_f = P.sbuf("scanm_f", [128, 512], F32)
    rowm = P.sbuf("rowm", [64, 16], F32)
    o_att = P.sbuf("o_att", [128, 8, 64], F32)
    PDO_ = P.sbuf("PDO0", [128, 2, 2, 512], BF16)
    PDO = [PDO_, PDO_]
    o_attn = P.sbuf("o_attn", [128, 512], BF16)
    kv_out = P.sbuf("kv_out", [128, 256], F32)

    tmp_cf = P.sbuf("tmp_cf", [128, 128], F32)
    rb = P.sbuf("rb", [32, 8], F32)
    oh = P.sbuf("oh", [32, 128], F32)
    Rm = P.sbuf("Rm", [32, 8, 128], F32)
    ones32 = P.sbuf("ones32", [32, 128], F32)
    Lb = P.sbuf("Lb", [128, 8, 383], F32)
    Tf = P.sbuf("Tf", [128, 8, 256], F32)
    bdn = P.sbuf("bdn", [64, 64], F32)

    def bcast_rows(t, n):
        return bass.AP(t, 0, [[0, 128], [1, n]])

    P.dma('sp', tmp_cf[:], c_ident.ap(), writes=['tmp_cf'])
    P.cp('dve', ident[:], tmp_cf[:], ['tmp_cf'], ['ident'])
    P.cp('dve', ident_f[:], tmp_cf[:], ['tmp_cf'], ['ident_f'])
    P.ms('pool', ones_bf[:], 1.0, ['ones_bf'])
    P.ms('pool', zeros_bf[:], 0.0, ['zeros_bf'])
    P.ms('pool', onesf8[:], 1.0, ['onesf8'])
    P.dma('sp', vec[:], vecs.ap(), writes=['vec'])
    P.dma('sp', fl[:], bcast_rows(flags, 16), writes=['fl'])

    w_in_v = w_in.ap().rearrange("(c p) n -> p c n", p=128)
    for (c0, c1, nm) in ((0, 768, 'w_bf_a'), (768, 1792, 'w_bf_b'), (1792, 2816, 'w_bf_c')):
        P.dma('pool', w_bf[:, :, c0:c1], w_in_v[:, :, c0:c1], writes=[nm])
    w_kd = P.sbuf("w_kd", [128, 8, 2, 128], BF16)
    P.dma('sp', xb[0][:128, :], xh.ap(), writes=['xb0'])
    WNAME = lambda col: 'w_bf_a' if col < 768 else ('w_bf_b' if col < 1792 else 'w_bf_c')

    if stop == 's1':
        return finish()
    P.dma('sp', maskbd[:], c_maskbd.ap(), writes=['maskbd'])
    P.dma('sp', masks[:], c_masks.ap(), writes=['masks'])
    P.dma('sp', scanm_f[:], bcast_rows(c_scanm, 512), writes=['scanm_f'])
    P.cp('dve', scanm[:], scanm_f[:], ['scanm_f'], ['scanm'])
    P.dma('sp', rowm[:], c_rowm.ap(), writes=['rowm'])
    P.dma('sp', esink[:], bcast_rows(sinks, 8), writes=['esink'])
    P.act(esink[:], esink[:], AF.Exp, ['esink'], ['esink'])
    P.tt('dve', stat[:, 0:4], lbg1, lbg0, ALU.subtract, ['vec'], ['stat'])
    P.act(stat[:, 0:4], stat[:, 0:4], AF.Exp, ['stat'], ['stat'])
    P.ts('dve', lb[:], stat[:, 0:4], 1.0, None, ALU.add, ALU.bypass, ['stat'], ['lb'])
    P.recip(lb[:], lb[:], ['lb'], ['lb'])
    P.tt('dve', stat[:, 4:8], stat[:, 0:4], lb[:], ALU.mult, ['stat', 'lb'], ['stat'])
    P.act(ln1mlb[:], stat[:, 4:8], AF.Ln, ['stat'], ['ln1mlb'])
    P.ms('pool', Vaug[:, :, :, 64:65], 1.0, ['Vaug'])
    P.ms('pool', S[:], 0.0, ['S'])
    P.ms('pool', Gtot[:], 0.0, ['Gtot'])

    if stop == 's2':
        return finish()
    P.dma('sp', rb[:], relb.ap(), writes=['rb'])
    P.dma('sp', oh[:], c_onehot.ap(), writes=['oh'])
    P.dma('sp', bdn[:], c_bdneg.ap(), writes=['bdn'])
    P.ms('pool', ones32[:], 1.0, ['ones32'])
    P.ms('pool', Lb[:], NEG * 8.0, ['Lb'])
    for h in range(8):
        P.ts('dve', Rm[:, h, :], oh[:], rb[:, h:h + 1], None, ALU.mult, ALU.bypass, ['oh', 'rb'], ['Rm'])
    for i in range(2):
        P.mm(bank_f[i][:, :], ones32[:], Rm[:, 4 * i:4 * i + 4, :], True, True, ['ones32', 'Rm'], [BK(i)])
        P.ts('dve', Lb[:, 4 * i:4 * i + 4, 127:255], bank_f[i][:, :].rearrange("p (h d) -> p h d", h=4), 8.0, None,
             ALU.mult, ALU.bypass, [BK(i), 'Lb'], ['Lb'])
    if stop == 's3':
        return finish()
    for h in range(8):
        P.dma('pool', bass.AP(scr, h * 128 * 383, [[383, 128], [1, 383]]), Lb[:, h, :], reads=['Lb'], writes=['scr%d' % h])
    for h in range(8):
        P.dma('pool', Tf[:, h, :], bass.AP(scr, h * 128 * 383 + 127, [[382, 128], [1, 256]]), reads=['scr%d' % h], writes=['Tf%d' % h])
    if stop == 's4':
        return finish()
    for kv in range(2):
        P.cp('dve', w_kd[:, :, kv, :].rearrange("p c (r d) -> p c r d", r=2),
             bc(w_bf[:, :, 512 + kv * 64: 512 + (kv + 1) * 64].unsqueeze(2), [128, 8, 2, 64]), ['w_bf_a'], ['w_kd'])
    xcnt = [0]
    FK = 1

    def fill(bank, k=FK):
        for _ in range(k):
            P.mm(bank_f[bank][:, :], zeros_bf[:, 0:128], zeros_bf[:, :], True, True, ['zeros_bf'], [BK(bank)], sig=False, skip=True)

    def rms_to_T(src_ap, nrows, gT, dst, dst_name, dst_off, xsrc_sb=None, fb=None, jb=7, junk=None):
        if xsrc_sb is None:
            i = xcnt[0] % 2
            xcnt[0] += 1
            xname = 'xb%d' % i
            P.dma('sp', xb[i][:nrows, :], src_ap, writes=[xname])
            xin = xb[i][:nrows, :]
        else:
            xin, xname = xsrc_sb
        jap, jname = junk if junk is not None else (xn, 'xn')
        P.act(jap[:nrows, :], xin, AF.Square, [xname], [jname, 'stat'], accum=stat[:nrows, 0:1])
        P.act(stat[:nrows, 1:2], stat[:nrows, 0:1], AF.Ln, ['stat'], ['stat'], scale=1.0 / 1024, bias=EPS)
        P.act(stat[:nrows, 2:3], stat[:nrows, 1:2], AF.Exp, ['stat'], ['stat'], scale=-0.5)
        P.ts('dve', xn[:nrows, :], xin, stat[:nrows, 2:3], None, ALU.mult, ALU.bypass, [xname, 'stat'], ['xn'])
        pb = bank_b[0].rearrange("p (c t) -> p c t", c=8)
        if fb is not None:
            fill(fb)
        for c in range(8):
            P.tr(pb[:, c, :nrows], xn[:nrows, c * 128:(c + 1) * 128], ident[:nrows, :nrows], ['xn', 'ident'], [BK(0)])
        P.tt('dve', dst[:, :, dst_off:dst_off + nrows], pb[:, :, :nrows], bc(gT.unsqueeze(2), [128, 8, nrows]),
             ALU.mult, [BK(0), 'vec'], [dst_name])

    def proj_fm(ps, lhs_fn, rhs_t, rhs_name, t0, n, wname, bk):
        for kc in range(8):
            P.mm(ps, lhs_fn(kc), rhs_t[:, kc, t0:t0 + n], kc == 0, kc == 7, [wname, rhs_name], [bk], sig=(kc == 7))

    def proj_tm(ps, lhs_t, lhs_name, t0, n, c0, ncols, bk):
        for kc in range(8):
            P.mm(ps, lhs_t[:, kc, t0:t0 + n], w_bf[:, kc, c0:c0 + ncols], kc == 0, kc == 7, [WNAME(c0), lhs_name], [bk],
                 sig=(kc == 7))

    def kv_block(t0, n, slot, out_kv=False):
        for kv in range(2):
            bk = 1 + kv
            proj_fm(bank_f[bk][:, :n], lambda kc, kv=kv: w_kd[:, kc, kv, :], hT, 'hT', t0, n, 'w_kd', BK(bk))
            P.cp('act', kT[:, kv, slot * 128: slot * 128 + n], bank_f[bk][:, :n], [BK(bk)], ['kT'])
        proj_tm(bank_f[3][:n, 0:256], hT, 'hT', t0, n, 512, 256, BK(3))
        P.cp('dve', Vaug[:n, slot, :, 0:64], bank_f[3][:n, 128:256].rearrange("p (k d) -> p k d", k=2), [BK(3)], ['Vaug'])
        if out_kv:
            P.cp('act', kv_out[:n, :], bank_f[3][:n, 0:256], [BK(3)], ['kv_out'])

    def attn_epilogue(nq, dst_off, fb=None):
        for hf in range(2):
            pv = bank_f[6 + hf][:nq, :].rearrange("p (h d) -> p h d", h=4)
            P.tt('dve', stat[:nq, 4:8], pv[:, :, 64], esink[:nq, 4 * hf:4 * hf + 4], ALU.add, [BK(6 + hf), 'esink'], ['stat'])
            P.recip(stat[:nq, 4:8], stat[:nq, 4:8], ['stat'], ['stat'])
            P.tt('dve', o_att[:nq, 4 * hf:4 * hf + 4, :], pv[:, :, 0:64], bc(stat[:nq, 4:8].unsqueeze(2), [nq, 4, 64]),
                 ALU.mult, [BK(6 + hf), 'stat'], ['o_att'])
        oa2 = o_att[:nq].rearrange("p h d -> p (h d)")
        P.act(o_attn[:nq, :], oa2, AF.Square, ['o_att'], ['o_attn', 'stat'], accum=stat[:nq, 0:1])
        P.act(stat[:nq, 1:2], stat[:nq, 0:1], AF.Ln, ['stat'], ['stat'], scale=1.0 / 512, bias=EPS)
        P.act(stat[:nq, 2:3], stat[:nq, 1:2], AF.Exp, ['stat'], ['stat'], scale=-0.5)
        P.ts('dve', o_attn[:nq, :], oa2, stat[:nq, 2:3], None, ALU.mult, ALU.bypass, ['o_att', 'stat'], ['o_attn'])
        pb = bank_b[0].rearrange("p (c t) -> p c t", c=8)
        if fb is not None:
            fill(fb)
        for c in range(4):
            P.tr(pb[:, c, :nq], o_attn[:nq, c * 128:(c + 1) * 128], ident[:nq, :nq], ['o_attn', 'ident'], [BK(0)])
        P.tt('dve', oaT[:, :, dst_off:dst_off + nq], pb[:, 0:4, :nq], bc(aogT.unsqueeze(2), [128, 4, nq]), ALU.mult,
             [BK(0), 'vec'], ['oaT'])

    def attn_scores(first, slot, ql, pb):
        tb = T8x if first else T8
        for kv in range(2):
            for par in range(2):
                bk = 2 + 2 * kv + par
                P.mm(bank_f[bk][:, :], ident[:], tb[:, 4 * kv + par:4 * kv + 4:2, :], True, False, ['ident', 'T8', 'T8x'], [BK(bk)],
                     sig=False, skip=True)
            for g in range(4):
                h = 4 * kv + g
                c, hf = h // 2, h % 2
                bk = 2 + 2 * kv + hf
                gi = g // 2
                qs = qT[hf * 64:(hf + 1) * 64, c, ql:ql + 128]
                P.mm(bank_f[bk][:, gi * 256:gi * 256 + 128], kT[hf * 64:(hf + 1) * 64, kv, slot * 128:(slot + 1) * 128], qs, False, False,
                     ['kT', 'qT'], [BK(bk)], sig=False, skip=True)
                P.mm(bank_f[bk][:, gi * 256 + 128:gi * 256 + 256], kT[hf * 64:(hf + 1) * 64, kv, (slot - 1) * 128:slot * 128], qs, False, g >= 2,
                     ['kT', 'qT'], [BK(bk)], sig=(g >= 2), skip=True)
            for par in range(2):
                bk = 2 + 2 * kv + par
                P.act(PDO[pb][:, kv, par, :], bank_f[bk][:, :], AF.Exp, [BK(bk)], ['PDO0_%d' % kv], scale=0.125)

    def attn_pv(slot, dst_off, pb):
        fill(1)
        for kv in range(2):
            for g in range(4):
                h = 4 * kv + g
                par, gi = g % 2, g // 2
                pv = bank_f[6 + h // 4][:, (h % 4) * 128:(h % 4) * 128 + 65]
                P.mm(pv, PDO[pb][:, kv, par, gi * 256 + 128:gi * 256 + 256], Vaug[:, slot - 1, kv, :], True, False, ['PDO0_%d' % kv, 'Vaug'],
                     [BK(6 + h // 4)], sig=False, skip=True)
                P.mm(pv, PDO[pb][:, kv, par, gi * 256:gi * 256 + 128], Vaug[:, slot, kv, :], False, True, ['PDO0_%d' % kv, 'Vaug'],
                     [BK(6 + h // 4)], skip=True)
        attn_epilogue(128, dst_off, fb=1)

    def interleave(*gens):
        gens = list(gens)
        while gens:
            for g in list(gens):
                try:
                    next(g)
                except StopIteration:
                    gens.remove(g)

    def hgrn_elem(n, tok_off, sample):
        tl = 4 if sample else 64
        nch = n // tl

        def head_gen(hh):
            if True:
                b0 = 1 if hh % 2 == 0 else 5
                yield
                f_ps = bank_f[b0]; q_ps = bank_f[b0 + 1]; g_ps = bank_f[b0 + 2]
                yield
                yield
                yield
                if not sample:
                    fill(4, 2 * FK)
                yield
                proj_fm(f_ps[:, :n], lambda kc, hh=hh: w_bf[:, kc, 1280 + hh * 128: 1280 + (hh + 1) * 128], hT, 'hT', 0, n, 'w_bf_b', BK(b0))
                yield
                proj_fm(q_ps[:, :n], lambda kc, hh=hh: w_bf[:, kc, 768 + hh * 128: 768 + (hh + 1) * 128], hT, 'hT', 0, n, 'w_bf_b', BK(b0 + 1))
                yield
                proj_fm(g_ps[:, :n], lambda kc, hh=hh: w_bf[:, kc, 2304 + hh * 128: 2304 + (hh + 1) * 128], hT, 'hT', 0, n, 'w_bf_c', BK(b0 + 2))
                yield
                tS = tA if hh % 2 == 0 else tB
                yield
                tN = 'tA' if hh % 2 == 0 else 'tB'
                yield
                ef, l1, l2, cbv, t4 = [t[:, :n] for t in tS]
                yield
                P.act(ef, f_ps[:, :n], AF.Exp, [BK(b0)], [tN + '0'], scale=-1.0)
                yield
                P.act(l1, ef, AF.Ln, [tN + '0'], [tN + '1'], bias=1.0)
                yield
                P.act(l2, ef, AF.Ln, [tN + '0', 'lb'], [tN + '2'], scale=lb[:, hh:hh + 1], bias=1.0)
                yield
                P.tt('dve', l2, l2, l1, ALU.subtract, [tN + '1', tN + '2'], [tN + '2'])
                yield
                if not sample:
                    P.scan(cbv, scanm[:, :n], l2, 0.0, ['scanm', tN + '2'], [tN + '3'])
                else:
                    l2v = l2.rearrange("p (b l) -> p b l", l=4); cb3 = cbv.rearrange("p (b l) -> p b l", l=4)
                    P.cp('dve', cb3[:, :, 0], l2v[:, :, 0], [tN + '2'], [tN + '3'])
                    for l in range(1, 4):
                        P.tt('dve', cb3[:, :, l], cb3[:, :, l - 1], l2v[:, :, l], ALU.add, [tN + '2', tN + '3'], [tN + '3'])
                yield
                P.tt('dve', t4, l1, cbv, ALU.add, [tN + '1', tN + '3'], [tN + '4'])
                yield
                P.stt(t4, f_ps[:, :n], -1.0, t4, ALU.mult, ALU.subtract, [BK(b0), tN + '4'], [tN + '4'])
                yield
                P.act(kt[:, hh, :n], t4, AF.Exp, [tN + '4', 'ln1mlb'], ['kt'], bias=ln1mlb[:, hh:hh + 1])
                yield
                cend = cbv.rearrange("p (c t) -> p c t", t=tl)[:, :, tl - 1]
                yield
                P.cp('dve', cbl[:, hh, :nch], cend, [tN + '3'], ['cbl'])
                yield
                P.act(dch[:, hh, :nch], cbl[:, hh, :nch], AF.Exp, ['cbl'], ['dch'])
                yield
                P.tt('pool', kh[:, hh, :n].rearrange("p (c t) -> p c t", t=tl), kt[:, hh, :n].rearrange("p (c t) -> p c t", t=tl),
                     bc(dch[:, hh, :nch].unsqueeze(2), [128, nch, tl]), ALU.mult, ['kt', 'dch'], ['kh'])
                yield
                P.act(ef, q_ps[:, :n], AF.Exp, [BK(b0 + 1)], [tN + '0'], scale=-1.0)
                yield
                P.act(l1, ef, AF.Ln, [tN + '0'], [tN + '1'], bias=1.0)
                yield
                P.tt('dve', l1, cbv, l1, ALU.subtract, [tN + '1', tN + '3'], [tN + '1'])
                yield
                P.act(l1, l1, AF.Exp, [tN + '1'], [tN + '1'])
                yield
                P.tt('dve', qt[:, hh, :n], q_ps[:, :n], l1, ALU.mult, [BK(b0 + 1), tN + '1'], ['qt'])
                yield
                if not sample:
                    P.scan(Gi[:, hh, :nch], onesf8[:, :nch], cbl[:, hh, :nch], Gtot[:, hh:hh + 1], ['cbl', 'Gtot', 'onesf8'], ['Gi'])
                    P.tt('dve', eG[:, hh, :nch], Gi[:, hh, :nch], cbl[:, hh, :nch], ALU.subtract, ['Gi', 'cbl'], ['eG'])
                    P.act(eG[:, hh, :nch], eG[:, hh, :nch], AF.Exp, ['eG'], ['eG'])
                    P.cp('dve', Gtot[:, hh:hh + 1], Gi[:, hh, nch - 1:nch], ['Gi'], ['Gtot'])
                    P.tt('pool', qg[:, hh, tok_off:tok_off + n].rearrange("p (c t) -> p c t", t=64),
                         qt[:, hh, :n].rearrange("p (c t) -> p c t", t=64), bc(eG[:, hh, :nch].unsqueeze(2), [128, nch, 64]), ALU.mult,
                         ['qt', 'eG'], ['qg'])
                yield
                P.act(ef, g_ps[:, :n], AF.Exp, [BK(b0 + 2)], [tN + '0'], scale=-1.0)
                yield
                P.ts('dve', ef, ef, 1.0, None, ALU.add, ALU.bypass, [tN + '0'], [tN + '0'])
                yield
                P.recip(ef, ef, [tN + '0'], [tN + '0'])
                yield
                P.stt(gate[:, hh, tok_off:tok_off + n], g_ps[:, :n], hggT[:, hh:hh + 1], ef, ALU.mult, ALU.mult, [BK(b0 + 2), tN + '0', 'vec'], ['gate'])
                yield

        interleave(head_gen(0), head_gen(1))
        interleave(head_gen(2), head_gen(3))
        for sb in range((n + 127) // 128):
            nn = min(128, n - sb * 128)
            proj_tm(bank_f[4][:nn, :], hT, 'hT', sb * 128, nn, 1792, 512, BK(4))
            P.cp('act', Vh[:nn, sb, :], bank_f[4][:nn, :], [BK(4)], ['Vh'])

    sbi = [0]

    def hgrn_pair(p, tok_off):
        c0 = p * 128
        A_ps = bank_f[5]; kT_ps = bank_b[6]; o_ps = bank_f[7]; U_ps = bank_f[4]
        for hh in range(4):
            P.mm(A_ps[:, hh * 128:(hh + 1) * 128], kt[:, hh, c0:c0 + 128], qt[:, hh, c0:c0 + 128], True, True, ['kt', 'qt'], [BK(5)],
                 sig=(hh == 3), skip=True)
        for hh in range(4):
            P.tr(kT_ps[:, hh * 128:(hh + 1) * 128], kh[:, hh, c0:c0 + 128], ident[:], ['kh', 'ident'], [BK(6)])
        P.tt('dve', Am[:], A_ps[:, :].rearrange("p (h t) -> p h t", h=4), bc(maskbd[:].unsqueeze(1), [128, 4, 128]), ALU.mult,
             [BK(5), 'maskbd'], ['Am'])
        P.cp('act', khT[:].rearrange("p h k -> p (h k)"), kT_ps[:, 0:512], [BK(6)], ['khT'])
        fill(0)
        P.mm(o_ps[:, :], zeros_bf[:, 0:128], zeros_bf[:, :], True, False, ['zeros_bf'], [BK(7)], sig=False, skip=True)
        for hh in range(4):
            P.mm(o_ps[:, hh * 128:(hh + 1) * 128], Vh[:, p, hh * 128:(hh + 1) * 128], Am[:, hh, :], False, False, ['Vh', 'Am'], [BK(7)],
                 sig=False, skip=True)
        U_bk = [4, 3]
        for ch in range(2):
            r0 = ch * 64
            for hh in range(4):
                P.mm(bank_f[U_bk[ch]][:, hh * 128:(hh + 1) * 128], khT[r0:r0 + 64, hh, :], Vh[r0:r0 + 64, p, hh * 128:(hh + 1) * 128], True, True,
                     ['khT', 'Vh'], [BK(U_bk[ch])], sig=(hh == 3), skip=True)
        for ch in range(2):
            sname = 'S_bf%d' % (sbi[0] % 2)
            sb_cur = S_bf[sbi[0] % 2]
            cidx = 2 * p + ch
            r0 = ch * 64
            fill(0)
            for hh in range(4):
                P.mm(o_ps[:, hh * 128 + r0: hh * 128 + r0 + 64], sb_cur[:, hh, :], qt[:, hh, c0 + r0:c0 + r0 + 64],
                     False, (ch == 1 and hh == 3), [sname, 'qt'], [BK(7)], sig=(ch == 1 and hh == 3), skip=True)
            P.tt('dve', S[:], S[:], bc(dch[:, :, cidx:cidx + 1], [128, 4, 128]), ALU.mult, ['S', 'dch'], ['S'])
            P.tt('dve', S[:], S[:], bank_f[U_bk[ch]][:, :].rearrange("p (h v) -> p h v", h=4), ALU.add, ['S', BK(U_bk[ch])], ['S'])
            sbi[0] += 1
            P.cp('dve', S_bf[sbi[0] % 2][:], S[:], ['S'], ['S_bf%d' % (sbi[0] % 2)])
        P.cp('act', o_loc[:, :, tok_off + c0: tok_off + c0 + 128], o_ps[:, :].rearrange("p (h t) -> p h t", h=4), [BK(7)], ['o_loc'])

    rms_to_T(None, 128, g1T, hT, 'hT', 0, xsrc_sb=(xb[0][:128, :], 'xb0'))
    xcnt[0] = 1
    if stop == 'h0n':
        return finish()
    kv_block(0, 128, 0)

    def tile_front(tt):
        for sb in range(4):
            r0 = tt * 512 + sb * 128
            rms_to_T(xp.ap()[r0:r0 + 128, :], 128, g1T, hT, 'hT', sb * 128, jb=1,
                     junk=((T8x[:].rearrange("p h q -> p (h q)")[:, 0:1024], 'T8x') if tt >= 1 else None))
        for c in range(4):
            bk = 4 + c % 2
            proj_fm(bank_f[bk][:, :], lambda kc, c=c: w_bf[:, kc, c * 128:(c + 1) * 128], hT, 'hT', 0, 512, 'w_bf_a', BK(bk))
            P.cp('act' if c % 2 else 'dve', qT[:, c, :], bank_f[bk][:, :], [BK(bk)], ['qT'])
        for sb in range(4):
            kv_block(sb * 128, 128, 1 + sb, out_kv=(tt == 3 and sb == 3))
        if tt == 3:
            P.dma('sp', nk_p.ap(), kv_out[:, 0:128], reads=['kv_out'], is_output=True)
            P.dma('sp', nv_p.ap(), kv_out[:, 128:256], reads=['kv_out'], is_output=True)

    tile_front(0)
    P.cp('dve', T8[:], Tf[:], ['Tf%d' % h_ for h_ in range(8)], ['T8'])
    P.cp('dve', T8x[:, :, 0:128], Tf[:, :, 0:128], ['Tf%d' % h_ for h_ in range(8)], ['T8x'])
    P.ts('dve', T8x[:, :, 128:256], Tf[:, :, 128:256], HALO, None, ALU.add, ALU.bypass, ['Tf%d' % h_ for h_ in range(8)] + ['fl', 'T8x'], ['T8x'])
    for h in range(8):
        P.tt('dve', T8n[:, h, :], Tf[0:64, h, 0:64], bdn[:], ALU.add, ['Tf%d' % h, 'bdn'], ['T8n'])
    P.barrier(skip_pool_dma=True)
    P.free('tmp_cf', 'rb', 'oh', 'Rm', 'ones32', 'Lb', 'Tf', 'bdn', 'scanm_f')

    if stop == 'setup':
        return finish()
    Vh = P.sbuf("Vh", [128, 4, 512], BF16)
    tA = [P.sbuf("tA%d" % i, [128, 512], F32) for i in range(5)]
    tB = [P.sbuf("tB%d" % i, [128, 512], F32) for i in range(5)]
    kt = P.sbuf("kt", [128, 4, 512], BF16)
    kh = P.sbuf("kh", [128, 4, 512], BF16)
    qt = P.sbuf("qt", [128, 4, 512], BF16)
    dch = P.sbuf("dch", [128, 4, 16], F32)
    eG = P.sbuf("eG", [128, 4, 8], F32)
    Gi = P.sbuf("Gi", [128, 4, 8], F32)
    cbl = P.sbuf("cbl", [128, 4, 16], F32)
    Am = P.sbuf("Am", [128, 4, 128], BF16)
    khT = P.sbuf("khT", [128, 4, 128], BF16)
    S_bf = [P.sbuf("S_bf%d" % i, [128, 4, 128], BF16) for i in range(2)]

    P.cp('act', S_bf[0][:], S[:], ['S'], ['S_bf0'])

    for tt in range(4):
        if tt > 0:
            tile_front(tt)
        for sb in range(4):
            attn_scores(tt == 0 and sb == 0, 1 + sb, sb * 128, 0)
            attn_pv(1 + sb, tt * 512 + sb * 128, 0)
        P.cp('act', kT[:, :, 0:128], kT[:, :, 512:640], ['kT'], ['kT'])
        P.cp('pool', Vaug[:, 0, :, :], Vaug[:, 4, :, :], ['Vaug'], ['Vaug'])
        if stop == 't%da' % tt:
            return finish()
        hgrn_elem(512, tt * 512, False)
        if stop == 't%de' % tt:
            return finish()
        for p in range(4):
            hgrn_pair(p, tt * 512)
        if stop == 't%dh' % tt:
            return finish()

    if stop == 'p1':
        return finish()
    P.cp('dve', cc1_sb[:, 0:512], S[:].rearrange("p h v -> p (h v)"), ['S'], ['cc1_sb'])
    P.act(cc1_sb[:, 512:516], Gtot[:], AF.Exp, ['Gtot'], ['cc1_sb'])
    P.dma('pool', cc1_in.ap(), cc1_sb[:], reads=['cc1_sb'], writes=['cc1_in'])
    P.collective(cc1_in, cc1_out, GROUPS, ['cc1_in'], ['cc1_out'])

    if stop == 'ex1':
        return finish()
    SO = 2048
    P.barrier()
    P.free('PDO0', 'kv_out', 'Am', 'khT', 'S_bf0', 'S_bf1', 'T8x', 'Gi', 'eG')
    kvn = P.sbuf("kvn", [64, 256], F32)
    ckb = [P.sbuf("ckb%d" % i, [128, 128], F32) for i in range(3)]
    ckbf = [P.sbuf("ckbf%d" % i, [128, 2, 2, 64], BF16) for i in range(3)]
    cvb = [P.sbuf("cvb%d" % i, [128, 128], F32) for i in range(3)]
    cva = [P.sbuf("cva%d" % i, [128, 2, 65], BF16) for i in range(3)]
    kcT = [P.sbuf("kcT%d" % i, [128, 2, 128], BF16) for i in range(3)]
    Pcm = [P.sbuf("Pcm%d" % i, [128, 8, 64], BF16) for i in range(2)]
    Pn = P.sbuf("Pn", [64, 8, 64], BF16)

    rms_to_T(xs.ap(), 64, g1T, hT, 'hT', 0)
    for c in range(4):
        bk = 4 + c % 2
        proj_fm(bank_f[bk][:, :64], lambda kc, c=c: w_bf[:, kc, c * 128:(c + 1) * 128], hT, 'hT', 0, 64, 'w_bf_a', BK(bk))
        P.cp('act' if c % 2 else 'dve', qT[:, c, 0:64], bank_f[bk][:, :64], [BK(bk)], ['qT'])
    kv_block(0, 64, 1)
    P.cp('act', kvn[:], bank_f[3][:64, 0:256], [BK(3)], ['kvn'])
    if stop == 'sa0':
        return finish()
    dstk = nk_s.ap().rearrange("b s d -> (b s) d")
    dstv = nv_s.ap().rearrange("b s d -> (b s) d")
    P.dma('sp', nk_s.ap()[:, 0:124, :], ck.ap()[:, 4:128, :], is_output=True)
    P.dma('sp', nv_s.ap()[:, 0:124, :], cv.ap()[:, 4:128, :], is_output=True)
    for b in range(16):
        P.dma('sp', dstk[b * 128 + 124: b * 128 + 128, :], kvn[4 * b:4 * b + 4, 0:128], reads=['kvn'], is_output=True)
        P.dma('sp', dstv[b * 128 + 124: b * 128 + 128, :], kvn[4 * b:4 * b + 4, 128:256], reads=['kvn'], is_output=True)

    if stop == 'sa1':
        return finish()
    T8n_v = T8n[:].rearrange("p (hp par) q -> p hp par q", par=2)
    Pn_v = Pn[:].rearrange("p (hp par) q -> p hp par q", par=2)
    for h in range(8):
        kv, c, hf = h // 4, h // 2, h % 2
        sn_ps = bank_f[2 - hf]
        P.mm(sn_ps[:64, (h // 2) * 64:(h // 2 + 1) * 64], kT[hf * 64:(hf + 1) * 64, kv, 128:192], qT[hf * 64:(hf + 1) * 64, c, 0:64], True, True,
             ['kT', 'qT'], [BK(2 - hf)], sig=(h >= 6), skip=True)
    for par in range(2):
        tsn = tA[par][:64, 0:256].rearrange("p (hp q) -> p hp q", hp=4)
        P.tt('dve', tsn, bank_f[2 - par][:64, 0:256].rearrange("p (hp q) -> p hp q", hp=4), T8n_v[:, :, par, :], ALU.add,
             [BK(2 - par), 'T8n'], ['tA%d' % par])
        P.act(Pn_v[:, :, par, :], tsn, AF.Exp, ['tA%d' % par], ['Pn'], scale=0.125)
    if stop == 'sa1a':
        return finish()
    for i in range(3):
        P.ms('pool', cva[i][:, :, 64:65], 1.0, ['cva%d' % i])
    for i in range(2):
        P.ms('pool', Pcm[i][:], 0.0, ['Pcm%d' % i])
    for hf in range(2):
        P.mm(bank_f[6 + hf][:, :], zeros_bf[:, 0:128], zeros_bf[:, :], True, False, ['zeros_bf'], [BK(6 + hf)], sig=False, skip=True)
    if stop == 'sa1b':
        return finish()
    for h in range(8):
        kv = h // 4
        pv = bank_f[6 + h // 4][:64, (h % 4) * 128:(h % 4) * 128 + 65]
        P.mm(pv, Pn[:, h, :], Vaug[:64, 1, kv, :], False, False, ['Pn', 'Vaug'], [BK(6 + h // 4)], sig=False, skip=True)
    if stop == 'sa2':
        return finish()
    def satt_gen(b):
        i = b % 3
        yield
        j = b % 2
        yield
        P.dma('sp', ckb[i][:], ck.ap()[b], writes=['ckb%d' % i])
        yield
        P.dma('sp', cvb[i][:], cv.ap()[b], writes=['cvb%d' % i])
        yield
        P.cp('dve', ckbf[i][:], bc(ckb[i][:].rearrange("p (k d) -> p k d", k=2).unsqueeze(2), [128, 2, 2, 64]), ['ckb%d' % i], ['ckbf%d' % i])
        yield
        P.cp('pool', cva[i][:, :, 0:64], cvb[i][:].rearrange("p (k d) -> p k d", k=2), ['cvb%d' % i], ['cva%d' % i])
        yield
        tbk = 0 if j == 0 else 5
        yield
        sb0 = 3 if j == 0 else 1
        yield
        tSc = tA if j == 0 else tB
        yield
        tScN = 'tA' if j == 0 else 'tB'
        yield
        tp = bank_b[tbk].rearrange("p (k s) -> p k s", k=8)
        yield
        for kv in range(2):
            P.tr(tp[:, kv, :], ckbf[i][:, kv, :, :].rearrange("p r d -> p (r d)"), ident[:], ['ckbf%d' % i, 'ident'], [BK(tbk)])
        yield
        P.cp('act', kcT[i][:], tp[:, 0:2, :], [BK(tbk)], ['kcT%d' % i])
        yield
        T8c_v = T8[:, :, 128:132].rearrange("p (hp par) l -> p hp par l", par=2)
        yield
        for h in range(8):
            kv, c, hf = h // 4, h // 2, h % 2
            P.mm(bank_f[sb0 + hf][:, (h // 2) * 4:(h // 2 + 1) * 4], kcT[i][hf * 64:(hf + 1) * 64, kv, :], qT[hf * 64:(hf + 1) * 64, c, 4 * b:4 * b + 4], True, True,
                 ['kcT%d' % i, 'qT'], [BK(sb0 + hf)], sig=(h >= 6), skip=True)
        yield
        if b >= 2:
            P.ms('pool', Pcm[j][:, :, 4 * (b - 2):4 * (b - 2) + 4], 0.0, ['Pcm%d' % j])
        yield
        Pcm_v = Pcm[j][:].rearrange("p (hp par) q -> p hp par q", par=2)
        yield
        for par in range(2):
            tsc = tSc[2 + par][:, 0:16].rearrange("p (hp l) -> p hp l", hp=4)
            P.tt('dve', tsc, bank_f[sb0 + par][:, 0:16].rearrange("p (hp l) -> p hp l", hp=4), T8c_v[:, :, par, :], ALU.add,
                 [BK(sb0 + par), 'T8'], [tScN + '%d' % (2 + par)])
            P.act(Pcm_v[:, :, par, 4 * b:4 * b + 4], tsc, AF.Exp, [tScN + '%d' % (2 + par)], ['Pcm%d' % j], scale=0.125)
        yield
        for h in range(8):
            kv = h // 4
            pv = bank_f[6 + h // 4][:64, (h % 4) * 128:(h % 4) * 128 + 65]
            last = (b == 15 and h % 4 == 3)
            P.mm(pv, Pcm[j][:, h, :], cva[i][:, kv, :], False, last, ['Pcm%d' % j, 'cva%d' % i], [BK(6 + h // 4)], sig=last or (h == 7), skip=True)
        yield

    for b0 in range(0, 16, 2):
        interleave(satt_gen(b0), satt_gen(b0 + 1))
    if stop == 'sa3':
        return finish()
    attn_epilogue(64, SO)

    if stop == 'sattn':
        return finish()
    P.barrier()
    P.free('kvn', 'ckb0', 'ckb1', 'ckbf0', 'ckbf1', 'cvb0', 'cvb1', 'cvb2', 'cva0', 'cva1', 'cva2', 'kcT0', 'kcT1', 'kcT2', 'Pcm0', 'Pcm1', 'Pn', 'ckb2', 'ckbf2')
    S0 = [P.sbuf("S0_%d" % i, [128, 4, 128], F32) for i in range(3)]
    S0b = [P.sbuf("S0b%d" % i, [128, 4, 128], BF16) for i in range(3)]
    khm = [P.sbuf("khm%d" % i, [64, 4, 128], BF16) for i in range(3)]
    Am_s = P.sbuf("Am_s", [64, 4, 64], BF16)
    khT_s = P.sbuf("khT_s", [64, 4, 128], BF16)
    hgrn_elem(64, SO, True)
    A_ps = bank_f[5]; kT_ps = bank_b[6]; o_ps = bank_f[7]; U_ps = bank_f[4]
    for hh in range(4):
        P.mm(A_ps[:64, hh * 64:(hh + 1) * 64], kt[:, hh, 0:64], qt[:, hh, 0:64], True, True, ['kt', 'qt'], [BK(5)], sig=(hh == 3), skip=True)
    for hh in range(4):
        P.tr(kT_ps[:64, hh * 128:(hh + 1) * 128], kh[:, hh, 0:64], ident[:], ['kh', 'ident'], [BK(6)])
    P.tt('dve', Am_s[:], A_ps[:64, 0:256].rearrange("p (h t) -> p h t", h=4), bc(masks[:].unsqueeze(1), [64, 4, 64]), ALU.mult,
         [BK(5), 'masks'], ['Am_s'])
    P.cp('act', khT_s[:].rearrange("p h k -> p (h k)"), kT_ps[:64, 0:512], [BK(6)], ['khT_s'])
    P.mm(o_ps[:, :], zeros_bf[:, 0:128], zeros_bf[:, :], True, False, ['zeros_bf'], [BK(7)], sig=False, skip=True)
    for hh in range(4):
        P.mm(o_ps[:, hh * 64:(hh + 1) * 64], Vh[:64, 0, hh * 128:(hh + 1) * 128], Am_s[:, hh, :], False, False, ['Vh', 'Am_s'], [BK(7)],
             sig=False, skip=True)
    sh_v = sh.ap().rearrange("b h k v -> b k h v")
    nh_v = nh_s.ap().rearrange("b h k v -> b k h v")
    def shg_gen(b):
        i = b % 3
        yield
        ub = 4 if b % 2 == 0 else 3
        yield
        P.dma('sp', S0[i][:], sh_v[b], writes=['S0_%d' % i])
        yield
        P.cp('act', S0b[i][:], S0[i][:], ['S0_%d' % i], ['S0b%d' % i])
        yield
        P.act(khm[i][:].rearrange("p h k -> p (h k)"), khT_s[:].rearrange("p h k -> p (h k)"), AF.Identity, ['khT_s', 'rowm'], ['khm%d' % i],
              scale=rowm[:, b:b + 1])
        yield
        fill(0)
        yield
        for hh in range(4):
            last = (b == 15 and hh == 3)
            P.mm(o_ps[:, hh * 64 + 4 * b: hh * 64 + 4 * b + 4], S0b[i][:, hh, :], qt[:, hh, 4 * b:4 * b + 4], False, last,
                 ['S0b%d' % i, 'qt'], [BK(7)], sig=last or hh == 3, skip=True)
        yield
        for hh in range(4):
            P.mm(bank_f[ub][:, hh * 128:(hh + 1) * 128], khm[i][:, hh, :], Vh[:64, 0, hh * 128:(hh + 1) * 128], True, True, ['khm%d' % i, 'Vh'], [BK(ub)],
                 sig=(hh == 3), skip=True)
        yield
        P.tt('dve', S0[i][:], S0[i][:], bc(dch[:, :, b:b + 1], [128, 4, 128]), ALU.mult, ['S0_%d' % i, 'dch'], ['S0_%d' % i])
        yield
        P.tt('dve', S0[i][:], S0[i][:], bank_f[ub][:, :].rearrange("p (h v) -> p h v", h=4), ALU.add, ['S0_%d' % i, BK(ub)], ['S0_%d' % i])
        yield
        P.dma('pool', nh_v[b], S0[i][:], reads=['S0_%d' % i], is_output=True)
        yield

    for b0 in range(0, 16, 2):
        interleave(shg_gen(b0), shg_gen(b0 + 1))
    P.cp('act', o_loc[:, :, SO:SO + 64], o_ps[:, 0:256].rearrange("p (h t) -> p h t", h=4), [BK(7)], ['o_loc'])

    if stop == 'shg':
        return finish()
    P.barrier()
    P.free('w_bf', 'w_kd', 'hT', 'kT', 'Vaug', 'qT', 'T8', 'T8n', 'esink', 'lb', 'ln1mlb', 'maskbd', 'masks', 'scanm', 'rowm', 'o_att',
           'o_attn', 'Vh', 'kt', 'kh', 'qt', 'dch', 'cbl', 'S0_0', 'S0_1', 'S0_2', 'S0b0', 'S0b1', 'S0b2', 'khm0', 'khm1', 'khm2', 'Am_s', 'khT_s')
    ohT = P.sbuf("ohT", [128, 4, NTOK], BF16)
    w_ob = P.sbuf("w_ob", [128, 8, 1024], BF16)
    ccr = P.sbuf("ccr", [128, 4, 516], F32)
    Sst = P.sbuf("Sst", [128, 4, 128], F32)
    Sst_bf = P.sbuf("Sst_bf", [128, 4, 128], BF16)
    alpha = P.sbuf("alpha", [128, 4], F32)
    sqb = [P.sbuf("sqb%d" % i, [128, 512], BF16) for i in range(4)]
    P.dma('pool', w_ob[:], w_out.ap().rearrange("(c p) n -> p c n", p=128), writes=['w_ob'])
    P.dma('sp', ccr[:], cc1_out.ap().rearrange("(r p) f -> p r f", p=128), reads=['cc1_out'], writes=['ccr'])
    P.ms('dve', Sst[:], 0.0, ['Sst'])
    for j in range(4):
        P.ts('dve', alpha[:], ccr[:, j, 512:516], ACTF[j], NACT[j], ALU.mult, ALU.add, ['ccr', 'fl'], ['alpha'])
        P.tt('dve', Sst[:], Sst[:], bc(alpha[:].unsqueeze(2), [128, 4, 128]), ALU.mult, ['Sst', 'alpha'], ['Sst'])
        P.stt(Sst[:].rearrange("p h v -> p (h v)"), ccr[:, j, 0:512], ACTF[j], Sst[:].rearrange("p h v -> p (h v)"), ALU.mult, ALU.add,
              ['ccr', 'fl', 'Sst'], ['Sst'])
    P.cp('act', Sst_bf[:], Sst[:], ['Sst'], ['Sst_bf'])
    P.tt('dve', Sst[:], Sst[:], bc(cc1_sb[:, 512:516].unsqueeze(2), [128, 4, 128]), ALU.mult, ['Sst', 'cc1_sb', 'Sst_bf'], ['Sst'])
    P.tt('dve', Sst[:].rearrange("p h v -> p (h v)"), Sst[:].rearrange("p h v -> p (h v)"), cc1_sb[:, 0:512], ALU.add, ['Sst', 'cc1_sb'], ['Sst'])
    P.dma('sp', nh_p.ap().rearrange("h k v -> k h v"), Sst[:], reads=['Sst'], is_output=True)

    tiles512 = [(i * 512, 512) for i in range(4)] + [(SO, 64)]
    tset = [(tA[0], 'tA0', tA[1], 'tA1'), (tA[2], 'tA2', tA[3], 'tA3'), (tB[0], 'tB0', tB[1], 'tB1'), (tB[2], 'tB2', tB[3], 'tB3')]
    for (t0, n) in tiles512:
        H = range(4)
        ot = [tset[h][0][:, :n] for h in H]; otn = [tset[h][1] for h in H]
        rs = [tset[h][2][:, :n] for h in H]; rsn = [tset[h][3] for h in H]
        bm = [bank_f[2 * h] for h in H]; bs = [bank_f[2 * h + 1] for h in H]
        for h in H:
            if t0 < SO:
                P.mm(bm[h][:, :n], Sst_bf[:, h, :], qg[:, h, t0:t0 + n], True, True, ['Sst_bf', 'qg'], [BK(2 * h)])
        for h in H:
            if t0 < SO:
                P.tt('dve', ot[h], bm[h][:, :n], o_loc[:, h, t0:t0 + n], ALU.add, [BK(2 * h), 'o_loc'], [otn[h]])
            else:
                P.cp('dve', ot[h], o_loc[:, h, t0:t0 + n], ['o_loc'], [otn[h]])
        for h in H:
            P.act(sqb[h][:, :n], ot[h], AF.Square, [otn[h]], ['sqb%d' % h])
        for h in H:
            P.mm(bs[h][:, :n], ones_bf[:], sqb[h][:, :n], True, True, ['ones_bf', 'sqb%d' % h], [BK(2 * h + 1)])
        for h in H:
            P.act(rs[h], bs[h][:, :n], AF.Ln, [BK(2 * h + 1)], [rsn[h]], scale=1.0 / 128, bias=EPS)
            P.act(rs[h], rs[h], AF.Exp, [rsn[h]], [rsn[h]], scale=-0.5)
        for h in H:
            P.tt('dve', ot[h], ot[h], rs[h], ALU.mult, [otn[h], rsn[h]], [otn[h]])
            P.tt('dve', ohT[:, h, t0:t0 + n], ot[h], gate[:, h, t0:t0 + n], ALU.mult, [otn[h], 'gate'], ['ohT'])
    if stop == 'p2a':
        return finish()

    P.barrier()
    P.free('o_loc', 'qg', 'gate', 'S', 'Gtot', 'cc1_sb', 'ccr', 'Sst', 'Sst_bf', 'alpha', 'sqb0', 'sqb1', 'sqb2', 'sqb3', 'tA0', 'tA1', 'tA2', 'tA3', 'tA4', 'tB0', 'tB1', 'tB2', 'tB3', 'tB4')
    h2T = P.sbuf("h2T", [128, 8, NTOK], BF16)
    x1 = [P.sbuf("x1_%d" % i, [128, 1024], F32) for i in range(17)]
    tiles128 = [(i * 128, 128) for i in range(16)] + [(SO, 64)]
    def outproj(ti):
        t0, n = tiles128[ti]
        i = xcnt[0] % 2
        xcnt[0] += 1
        src = xp.ap()[t0:t0 + n, :] if t0 < SO else xs.ap()
        P.dma('sp', xb[i][:n, :], src, writes=['xb%d' % i])
        for hf in range(2):
            bk = 2 + hf + 2 * (ti % 2)
            ps = bank_f[bk]
            for c in range(8):
                lhs = oaT[:, c, t0:t0 + n] if c < 4 else ohT[:, c - 4, t0:t0 + n]
                P.mm(ps[:n, :], lhs, w_ob[:, c, hf * 512:(hf + 1) * 512], c == 0, c == 7, ['oaT', 'ohT', 'w_ob'], [BK(bk)], sig=(c == 7))
            P.tt('dve', x1[ti][:n, hf * 512:(hf + 1) * 512], ps[:n, :], xb[i][:n, hf * 512:(hf + 1) * 512], ALU.add,
                 [BK(bk), 'xb%d' % i], ['x1_%d' % ti])

    def norm2(ti):
        t0, n = tiles128[ti]
        rms_to_T(None, n, g2T, h2T, 'h2T', t0, xsrc_sb=(x1[ti][:n, :], 'x1_%d' % ti), jb=7)

    for ti in range(17):
        outproj(ti)
        if ti > 1:
            norm2(ti - 2)
    norm2(15)
    norm2(16)
    if stop == 'p2b':
        return finish()
    P.barrier()
    P.free('oaT', 'ohT', 'w_ob', 'xn')
    fg = P.sbuf("fg", [128, 1024], F32)
    P.dma('sp', fg[:], bcast_rows(fgv, 1024), writes=['fg'])
    uT = P.sbuf("uT", [128, 8, NTOK], BF16)
    wfo = P.sbuf("wfo", [128, 8, 1024], BF16)
    wfi = [P.sbuf("wfi%d" % i, [128, 2, 8, 128], BF16) for i in range(3)]
    a_buf = P.sbuf("a_buf", [128, 2050], F32)
    cT = [P.sbuf("cT%d" % i, [128, 512], F32) for i in range(4)]
    ccnt = [0]
    a_last = P.sbuf("a_last", [128, 2, 22], F32)
    a_first = P.sbuf("a_first", [128, 8, 2], F32)
    g_first = P.sbuf("g_first", [128, 8, 2], F32)
    prevT = P.sbuf("prevT", [128, 2, 8], F32)
    fx = [P.sbuf("fx%d" % i, [128, 8, 2], F32) for i in range(2)]
    ccr2 = P.sbuf("ccr2", [128, 4, 16], F32)
    cc2_sb = P.sbuf("cc2_sb", [128, 16], F32)
    zb = P.sbuf("zb", [128, 16, 6], F32)
    prev_s = P.sbuf("prev_s", [128, 8, 32], F32)
    a_smp = P.sbuf("a_smp", [128, 8, 16, 2], F32)
    sct = P.sbuf("sct", [64, 1024], F32)
    ncs = sct
    yt = P.sbuf("yt", [128, 1024], F32)
    P.ms('pool', a_buf[:, 0:2], 0.0, ['a_buf'])
    w_fi_v = w_fi.ap().rearrange("(c p) n -> p c n", p=128)
    w_fo_v = w_fo.ap().rearrange("(c p) n -> p c n", p=128)
    wcnt = [0]
    f1cnt = [0]
    f2cnt = [0]

    for pi, (c_lo, c_hi) in enumerate(PARTS):
        ncp = c_hi - c_lo
        last_part = (pi == len(PARTS) - 1)
        P.dma('sp', sct[:32, 0:ncp * 128], scv.ap()[:, c_lo * 128:c_hi * 128], writes=['sct'])
        for cl in range(ncp):
            P.op('pe', lambda e, cl=cl: e.transpose(out=bank_f[0][:, cl * 32:(cl + 1) * 32], in_=sct[:32, cl * 128:(cl + 1) * 128],
                                                    identity=ident_f[:32, :32]), ['sct', 'ident_f'], [BK(0)])
        P.cp('act', prev_s[:, 0:ncp, :], bank_f[0][:, 0:ncp * 32].rearrange("p (c q) -> p c q", c=ncp), [BK(0)], ['prev_s'])
        for cl in range(ncp):
            c = c_lo + cl
            wi = c % 3
            wn = 'wfi%d' % wi
            for cn in ([0, 1] if c == 0 else []) + ([c + 2] if c + 2 < 22 else []):
                P.dma('pool', wfi[cn % 3][:, 0, :, :], w_fi_v[:, :, cn * 128:(cn + 1) * 128], writes=['wfi%d' % (cn % 3)])
                P.dma('pool', wfi[cn % 3][:, 1, :, :], w_fi_v[:, :, 2816 + cn * 128: 2816 + (cn + 1) * 128], writes=['wfi%d' % (cn % 3)])
            if cl == 0:
                P.dma('pool', wfo[:, 0:ncp, :], w_fo_v[:, c_lo:c_hi, :], writes=['wfo'])
            w0, w1, w2 = cwT[0][:, c:c + 1], cwT[1][:, c:c + 1], cwT[2][:, c:c + 1]
            bcol = cbT[:, c:c + 1]
            def tile_gen(t0, n):
                pset = 2 + 2 * (f1cnt[0] % 3)
                f1cnt[0] += 1
                pa = bank_f[pset]; pg = bank_f[pset + 1]
                pan = BK(pset); pgn = BK(pset + 1)
                for kc in range(8):
                    P.mm(pa[:, :n], wfi[wi][:, 0, kc, :], h2T[:, kc, t0:t0 + n], kc == 0, kc == 7, [wn, 'h2T'], [pan], sig=(kc == 7))
                yield
                for kc in range(8):
                    P.mm(pg[:, :n], wfi[wi][:, 1, kc, :], h2T[:, kc, t0:t0 + n], kc == 0, kc == 7, [wn, 'h2T'], [pgn], sig=(kc == 7))
                yield
                ci = ccnt[0] % 2
                ccnt[0] += 1
                t1 = cT[ci][:, :n]; sl = cT[2 + ci][:, :n]
                t1n = 'cT%d' % ci; sln = 'cT%d' % (2 + ci)
                if t0 < SO:
                    abn = 'a_buf%d' % (t0 // 512); abp = 'a_buf%d' % (t0 // 512 - 1) if t0 > 0 else 'a_buf'
                    P.cp('act', a_buf[:, 2 + t0:2 + t0 + n], pa[:, :n], [pan], [abn])
                    yield
                    P.act(t1, a_buf[:, t0:t0 + n], AF.Identity, [abn, abp, 'vec'], [t1n], scale=w0, bias=bcol)
                    yield
                    P.stt(t1, a_buf[:, t0 + 1:t0 + 1 + n], w1, t1, ALU.mult, ALU.add, [abn, abp, 'vec', t1n], [t1n])
                    yield
                    P.stt(t1, pa[:, :n], w2, t1, ALU.mult, ALU.add, [pan, 'vec', t1n], [t1n])
                    yield
                    P.act(sl, t1, AF.Silu, [t1n], [sln])
                    yield
                    P.tt('dve', uT[:, cl, t0:t0 + n], sl, pg[:, :n], ALU.mult, [sln, pgn], ['uT'])
                    yield
                    if t0 == 0:
                        P.cp('dve', a_first[:, cl, :], a_buf[:, 2:4], ['a_buf0'], ['a_first'])
                        P.cp('act', g_first[:, cl, :], pg[:, 0:2], [pgn], ['g_first'])
                    if t0 == 1536:
                        P.cp('dve', a_last[:, :, c], a_buf[:, 2048:2050], ['a_buf3'], ['a_last'])
                else:
                    P.cp('dve', zb[:, :, 0:2], prev_s[:, cl, :].rearrange("p (b j) -> p b j", j=2), ['prev_s'], ['zb'])
                    P.cp('act', zb[:, :, 2:6], pa[:, :64].rearrange("p (b l) -> p b l", l=4), [pan], ['zb'])
                    t3 = t1.rearrange("p (b l) -> p b l", l=4)
                    P.ts('dve', t3, zb[:, :, 0:4], w0, bcol, ALU.mult, ALU.add, ['zb', 'vec'], [t1n])
                    P.stt(t3, zb[:, :, 1:5], w1, t3, ALU.mult, ALU.add, ['zb', 'vec', t1n], [t1n])
                    P.stt(t3, zb[:, :, 2:6], w2, t3, ALU.mult, ALU.add, ['zb', 'vec', t1n], [t1n])
                    P.act(sl, t1, AF.Silu, [t1n], [sln])
                    P.tt('dve', uT[:, cl, t0:t0 + n], sl, pg[:, :n], ALU.mult, [sln, pgn], ['uT'])
                    P.cp('dve', a_smp[:, cl, :, :], zb[:, :, 4:6], ['zb'], ['a_smp'])

            interleave(tile_gen(*tiles512[0]), tile_gen(*tiles512[1]))
            interleave(tile_gen(*tiles512[2]), tile_gen(*tiles512[3]))
            interleave(tile_gen(*tiles512[4]))
        P.cp('dve', cc2_sb[:].rearrange("p (j c) -> p j c", j=2)[:, :, 0:ncp], a_last[:, :, c_lo:c_hi], ['a_last'], ['cc2_sb'])
        P.dma('pool', cc2_in[pi].ap(), cc2_sb[:], reads=['cc2_sb'], writes=['cc2_in%d' % pi])
        P.collective(cc2_in[pi], cc2_out[pi], GROUPS, ['cc2_in%d' % pi], ['cc2_out%d' % pi])
        P.dma('sp', ccr2[:], cc2_out[pi].ap().rearrange("(r p) f -> p r f", p=128), reads=['cc2_out%d' % pi], writes=['ccr2'])
        for cl in range(ncp):
            P.op('pe', lambda e, cl=cl: e.transpose(out=bank_f[0][:32, cl * 128:(cl + 1) * 128] if cl < 4 else bank_f[1][:32, (cl - 4) * 128:(cl - 3) * 128],
                                                    in_=a_smp[:, cl, :, :].rearrange("p b j -> p (b j)"), identity=ident_f[:, :]),
                 ['a_smp', 'ident_f'], [BK(0) if cl < 4 else BK(1)])
        P.cp('act', ncs[:32, 0:min(ncp, 4) * 128], bank_f[0][:32, 0:min(ncp, 4) * 128], [BK(0)], ['sct'])
        if ncp > 4:
            P.cp('act', ncs[:32, 512:ncp * 128], bank_f[1][:32, 0:(ncp - 4) * 128], [BK(1)], ['sct'])
        P.dma('sp', nc_s.ap()[:, c_lo * 128:c_hi * 128], ncs[:32, 0:ncp * 128], reads=['sct'], is_output=True)
        def ffn_out_tile(ti):
            t0, n = tiles128[ti]
            fset = 2 * (f2cnt[0] % 2)
            f2cnt[0] += 1
            for hf in range(2):
                ps = bank_f[2 + hf + fset]
                for cl in range(ncp):
                    P.mm(ps[:n, :], uT[:, cl, t0:t0 + n], wfo[:, cl, hf * 512:(hf + 1) * 512], cl == 0, cl == ncp - 1, ['uT', 'wfo'],
                         [BK(2 + hf + fset)], sig=(cl == ncp - 1))
                if not last_part:
                    P.tt('dve', x1[ti][:n, hf * 512:(hf + 1) * 512], ps[:n, :], x1[ti][:n, hf * 512:(hf + 1) * 512], ALU.add,
                         [BK(2 + hf + fset), 'x1_%d' % ti], ['x1_%d' % ti])
                else:
                    P.tt('dve', yt[:n, hf * 512:(hf + 1) * 512], ps[:n, :], x1[ti][:n, hf * 512:(hf + 1) * 512], ALU.add,
                         [BK(2 + hf + fset), 'x1_%d' % ti], ['yt'])
            if last_part:
                xo = xb[ti % 2]; xon = 'xb%d' % (ti % 2)
                P.act(xo[:n, :], yt[:n, :], AF.Square, ['yt'], [xon, 'stat'], accum=stat[:n, 0:1])
                P.act(stat[:n, 1:2], stat[:n, 0:1], AF.Ln, ['stat'], ['stat'], scale=1.0 / 1024, bias=EPS)
                P.act(stat[:n, 2:3], stat[:n, 1:2], AF.Exp, ['stat'], ['stat'], scale=-0.5)
                P.stt(xo[:n, :], yt[:n, :], stat[:n, 2:3], fg[:n, :], ALU.mult, ALU.mult, ['yt', 'stat', 'fg'], [xon])
                dst = y_p.ap()[t0:t0 + n, :] if t0 < SO else y_s.ap()
                P.dma('sp', dst, xo[:n, :], reads=[xon], is_output=True)

        for ti in range(1, 17):
            ffn_out_tile(ti)
        pTf = prevT[:].rearrange("p j c -> p (j c)")
        P.ts('dve', pTf, ccr2[:, 0, :], SEL[0], None, ALU.mult, ALU.bypass, ['ccr2', 'fl'], ['prevT'])
        for j in range(1, 4):
            P.stt(pTf, ccr2[:, j, :], SEL[j], pTf, ALU.mult, ALU.add, ['ccr2', 'fl', 'prevT'], ['prevT'])
        W = [cwT[j][:, c_lo:c_hi] for j in range(3)]
        bcs = cbT[:, c_lo:c_hi]
        f0 = fx[0][:, 0:ncp, :]; f1 = fx[1][:, 0:ncp, :]
        P.tt('dve', f0[:, :, 0], prevT[:, 0, 0:ncp], W[0], ALU.mult, ['prevT', 'vec'], ['fx0'])
        P.tt('dve', f0[:, :, 1], prevT[:, 1, 0:ncp], W[0], ALU.mult, ['prevT', 'vec', 'fx0'], ['fx0'])
        P.tt('dve', f1[:, :, 0], prevT[:, 1, 0:ncp], W[1], ALU.mult, ['prevT', 'vec'], ['fx1'])
        P.tt('dve', f1[:, :, 1], a_first[:, 0:ncp, 0], W[1], ALU.mult, ['a_first', 'vec', 'fx1'], ['fx1'])
        P.tt('dve', f0, f0, f1, ALU.add, ['fx0', 'fx1'], ['fx0'])
        P.tt('dve', f1[:, :, 0], a_first[:, 0:ncp, 0], W[2], ALU.mult, ['a_first', 'vec', 'fx1'], ['fx1'])
        P.tt('dve', f1[:, :, 1], a_first[:, 0:ncp, 1], W[2], ALU.mult, ['a_first', 'vec', 'fx1'], ['fx1'])
        P.tt('dve', f0, f0, f1, ALU.add, ['fx0', 'fx1'], ['fx0'])
        P.tt('dve', f0, f0, bc(bcs.unsqueeze(2), [128, ncp, 2]), ALU.add, ['fx0', 'vec'], ['fx0'])
        P.act(f1, f0, AF.Silu, ['fx0', 'fx1'], ['fx1'])
        P.tt('dve', uT[:, 0:ncp, 0:2], f1, g_first[:, 0:ncp, :], ALU.mult, ['fx1', 'g_first'], ['uT'])
        ffn_out_tile(0)

    P.op('pe', lambda e: e.transpose(out=bank_f[0][:44, 0:128], in_=a_last[:].rearrange("p j c -> p (j c)"), identity=ident_f[:, :]),
         ['a_last', 'ident_f'], [BK(0)])
    P.cp('act', ncs[:44, 0:128], bank_f[0][:44, 0:128], [BK(0)], ['sct'])
    for j in range(2):
        P.dma('sp', nc_p.ap()[j].rearrange("(c p) -> c p", p=128), ncs[j * 22:(j + 1) * 22, 0:128], reads=['sct'], is_output=True)

    P.emit()
    P.close()
    return nc


def _t5_bucket(dist):
    import math
    n = np.maximum(dist, 0)
    max_exact = 16
    nf = np.maximum(n, 1).astype(np.float32)
    large = max_exact + (np.log(nf / max_exact) / math.log(128 / max_exact) * (32 - max_exact)).astype(np.int32)
    large = np.minimum(large, 31)
    return np.where(n < max_exact, n, large)


def _constants():
    c = {}
    c["c_ident"] = np.eye(128, dtype=np.float32)
    oh = np.zeros((32, 128), np.float32)
    bk = _t5_bucket(np.arange(128))
    oh[bk, np.arange(128)] = 1.0
    c["c_onehot"] = oh
    s = np.arange(128)[:, None]; t = np.arange(128)[None, :]
    c["c_maskbd"] = ((s // 64 == t // 64) & (s <= t)).astype(np.float32)
    s = np.arange(64)[:, None]; t = np.arange(64)[None, :]
    c["c_masks"] = ((s // 4 == t // 4) & (s <= t)).astype(np.float32)
    c["c_bdneg"] = np.where(s // 4 == t // 4, 0.0, NEG * 8.0).astype(np.float32)
    m = np.ones((1, 512), np.float32); m[0, ::64] = 0.0
    c["c_scanm"] = m
    c["c_rowm"] = (np.arange(64)[:, None] // 4 == np.arange(16)[None, :]).astype(np.float32)
    return c


_CACHE = {}


def kernel(x_prompt, x_sample, cache_k_win, cache_v_win, state_hgrn, state_conv,
           norm1_g, w_in, attn_sinks, rel_bias, lb_gamma, attn_out_g, hg_out_g, w_out,
           norm2_g, w_ffn_in, conv_w, conv_b, w_ffn_out, final_g):
    f = lambda a: np.ascontiguousarray(np.asarray(a, dtype=np.float32))
    x_prompt, x_sample = f(x_prompt), f(x_sample)
    ckw, cvw, sth, stc = f(cache_k_win)[0], f(cache_v_win)[0], f(state_hgrn)[0], f(state_conv)[0]
    fm = lambda v, n: f(v).reshape(n, 128).T
    vecs = np.concatenate([fm(norm1_g[0], 8), fm(norm2_g[0], 8), fm(attn_out_g[0], 4), fm(hg_out_g[0], 4),
                           fm(lb_gamma[0], 4), fm(lb_gamma[1], 4),
                           fm(conv_w[0][0], 22), fm(conv_w[0][1], 22), fm(conv_w[0][2], 22), fm(conv_b[0], 22)], axis=1)
    vecs = np.ascontiguousarray(vecs, dtype=np.float32)
    shared = {
        "w_in": f(w_in)[0], "w_out": f(w_out)[0], "w_fi": f(w_ffn_in)[0], "w_fo": f(w_ffn_out)[0],
        "vecs": vecs, "fgv": f(final_g).reshape(1, 1024), "sinks": f(attn_sinks).reshape(1, 8), "relb": f(rel_bias),
    }
    shared.update(_constants())
    in_maps = []
    for c in range(8):
        b, r = c // 4, c % 4
        fl = np.zeros((1, 16), np.float32)
        fl[0, 0] = 0.0 if r > 0 else NEG * 8.0
        for j in range(4):
            fl[0, 1 + j] = 1.0 if j < r else 0.0
            fl[0, 5 + j] = 0.0 if j < r else 1.0
            fl[0, 9 + j] = 1.0 if j == r - 1 else 0.0
        m = dict(shared)
        m["xp"] = x_prompt[b, r * 2048:(r + 1) * 2048]
        m["xh"] = x_prompt[b, r * 2048 - 128:r * 2048] if r > 0 else np.zeros((128, 1024), np.float32)
        m["xs"] = x_sample[16 * c:16 * c + 16].reshape(64, 1024)
        m["ck"] = ckw[16 * c:16 * c + 16].reshape(16, 128, 128)
        m["cv"] = cvw[16 * c:16 * c + 16].reshape(16, 128, 128)
        m["sh"] = sth[16 * c:16 * c + 16]
        m["scv"] = stc[16 * c:16 * c + 16].reshape(32, 2816)
        m["flags"] = fl
        in_maps.append({k: np.ascontiguousarray(v) for k, v in m.items()})
    if "nc" not in _CACHE:
        _CACHE["nc"] = build_program()
    res = run_bass_kernel_spmd(_CACHE["nc"], in_maps, core_ids=list(range(8)))
    R = res.results
    y_prompt = np.stack([np.concatenate([R[4 * b + r]["y_p"] for r in range(4)], 0) for b in range(2)], 0)
    y_sample = np.concatenate([R[c]["y_s"].reshape(16, 4, 1024) for c in range(8)], 0)
    nk_p = np.stack([R[4 * b + 3]["nk_p"].reshape(128, 2, 64) for b in range(2)], 0)[None]
    nv_p = np.stack([R[4 * b + 3]["nv_p"].reshape(128, 2, 64) for b in range(2)], 0)[None]
    nh_p = np.stack([R[4 * b + 3]["nh_p"] for b in range(2)], 0)[None]
    nc_p = np.stack([R[4 * b + 3]["nc_p"] for b in range(2)], 0)[None]
    nk_s = np.concatenate([R[c]["nk_s"].reshape(16, 128, 2, 64) for c in range(8)], 0)[None]
    nv_s = np.concatenate([R[c]["nv_s"].reshape(16, 128, 2, 64) for c in range(8)], 0)[None]
    nh_s = np.concatenate([R[c]["nh_s"] for c in range(8)], 0)[None]
    nc_s = np.concatenate([R[c]["nc_s"].reshape(16, 2, 2816) for c in range(8)], 0)[None]
    outs = (y_prompt, y_sample, nk_p, nv_p, nh_p, nc_p, nk_s, nv_s, nh_s, nc_s)
    return tuple(np.ascontiguousarray(o, dtype=np.float32) for o in outs)
```
